# Optimizing a Trainium2 kernel written in Bass

```python
import jax, jax.numpy as jnp
from jax import lax
import numpy as np

D_MODEL = 1024
BATCH = 4
SEQ = 4096
DEPTH = 2
DEC_BATCH = 128
DEC_SEQ = 8
PAST_LEN = 8192
PAGE_SIZE = 128

D_MIX = D_MODEL
CONV_W = D_MIX // 2
CONV_K = 3
HEAD_DIM = 64
N_HEADS = (D_MIX - CONV_W) // HEAD_DIM
N_KV_HEADS = 2
GQA_G = N_HEADS // N_KV_HEADS
WINDOW = 128
BLOCK = 128
ROT_DIM = HEAD_DIM // 4
ROPE_THETA = 500000.0
EPS = 1e-6
ATTN_W = N_HEADS * HEAD_DIM
KV_W = N_KV_HEADS * HEAD_DIM
PROJ_SIZES = [CONV_W, CONV_W, CONV_W, CONV_W, ATTN_W, KV_W, KV_W, ATTN_W]
PROJ_OUT = sum(PROJ_SIZES)
PROJ_SPLITS = list(np.cumsum(PROJ_SIZES)[:-1].tolist())
NEG = -1e30

kernel_name = "hymba_conv_swa_adaln_decoder_step"


def rmsnorm(x, g):
    x32 = x.astype(jnp.float32)
    y = x32 * lax.rsqrt(jnp.mean(x32 * x32, axis=-1, keepdims=True) + EPS)
    return (y * g.astype(jnp.float32)).astype(x.dtype)


def rope_partial(x, pos):
    inv_freq = ROPE_THETA ** (-jnp.arange(0, ROT_DIM, 2, dtype=jnp.float32) / ROT_DIM)
    ang = pos[:, None] * inv_freq[None, :]
    cos = jnp.cos(ang)[None, :, None, :].astype(x.dtype)
    sin = jnp.sin(ang)[None, :, None, :].astype(x.dtype)
    xr, xp = x[..., :ROT_DIM], x[..., ROT_DIM:]
    x1, x2 = jnp.split(xr, 2, axis=-1)
    return jnp.concatenate([x1 * cos - x2 * sin, x2 * cos + x1 * sin, xp], axis=-1)


def sink_probs(s, sink, mask):
    s = jnp.where(mask, s, NEG)
    sk = sink.astype(jnp.float32)[:, :, None]
    m = jnp.maximum(jnp.max(s, axis=-1), sk)
    p = jnp.exp(s - m[..., None])
    denom = jnp.sum(p, axis=-1) + jnp.exp(sk - m)
    return p / denom[..., None]


def window_attn_prompt(q, k, v, sink):
    N, L = q.shape[0], q.shape[1]
    nb = L // BLOCK
    qb = q.astype(jnp.float32).reshape(N, nb, BLOCK, N_KV_HEADS, GQA_G, HEAD_DIM)
    pad = ((0, 0), (BLOCK, 0), (0, 0), (0, 0))
    kp = jnp.pad(k.astype(jnp.float32), pad).reshape(N, nb + 1, BLOCK, N_KV_HEADS, HEAD_DIM)
    vp = jnp.pad(v.astype(jnp.float32), pad).reshape(N, nb + 1, BLOCK, N_KV_HEADS, HEAD_DIM)
    kb = jnp.concatenate([kp[:, :-1], kp[:, 1:]], axis=2)
    vb = jnp.concatenate([vp[:, :-1], vp[:, 1:]], axis=2)
    s = jnp.einsum('nbqkgd,nbskd->nbkgqs', qb, kb) * (HEAD_DIM ** -0.5)
    qi = jnp.arange(BLOCK)[:, None] + BLOCK
    kj = jnp.arange(2 * BLOCK)[None, :]
    blk = jnp.arange(nb)[:, None, None]
    mask = (kj <= qi) & (kj > qi - WINDOW)
    mask = mask[None] & (blk * BLOCK + kj[None] - BLOCK >= 0)
    p = sink_probs(s, sink.reshape(N_KV_HEADS, GQA_G), mask[None, :, None, None])
    o = jnp.einsum('nbkgqs,nbskd->nbqkgd', p, vb)
    return o.reshape(N, L, ATTN_W)


def window_attn_cached(q, k_ext, v_ext, sink):
    N, L = q.shape[0], q.shape[1]
    Wb = k_ext.shape[1] - L
    qg = q.astype(jnp.float32).reshape(N, L, N_KV_HEADS, GQA_G, HEAD_DIM)
    s = jnp.einsum('nqkgd,nskd->nkgqs', qg, k_ext.astype(jnp.float32)) * (HEAD_DIM ** -0.5)
    qi = Wb + jnp.arange(L)[:, None]
    kj = jnp.arange(Wb + L)[None, :]
    mask = (kj <= qi) & (kj > qi - WINDOW)
    p = sink_probs(s, sink.reshape(N_KV_HEADS, GQA_G), mask)
    o = jnp.einsum('nkgqs,nskd->nqkgd', p, v_ext.astype(jnp.float32))
    return o.reshape(N, L, ATTN_W)


def layer(x, c, pos, conv_prev, k_prev, v_prev, w_mod, b_mod, norm_g, w_in, conv_w,
          q_norm_g, k_norm_g, sink, w_out):
    N, L, _ = x.shape
    mod = jax.nn.silu(c) @ w_mod + b_mod
    shift, scale, gate = jnp.split(mod, 3, axis=-1)
    h = rmsnorm(x, norm_g) * (1.0 + scale[:, None]) + shift[:, None]
    proj = h @ w_in
    gb, gc, hc, zc, q, k, v, za = jnp.split(proj, PROJ_SPLITS, axis=-1)

    u = gc * hc
    if conv_prev is None:
        conv_prev = jnp.zeros((N, CONV_K - 1, CONV_W), u.dtype)
    u_ext = jnp.concatenate([conv_prev.astype(u.dtype), u], axis=1)
    conv = sum(u_ext[:, i:i + L] * conv_w[i] for i in range(CONV_K))
    y_conv = gb * conv * jax.nn.silu(zc)
    new_conv = u_ext[:, -(CONV_K - 1):]

    q = rope_partial(rmsnorm(q.reshape(N, L, N_HEADS, HEAD_DIM), q_norm_g), pos)
    k = rope_partial(rmsnorm(k.reshape(N, L, N_KV_HEADS, HEAD_DIM), k_norm_g), pos)
    v = v.reshape(N, L, N_KV_HEADS, HEAD_DIM)
    if k_prev is None:
        attn = window_attn_prompt(q, k, v, sink)
        keep = min(WINDOW, L)
        new_k, new_v = k[:, -keep:], v[:, -keep:]
    else:
        Wb = k_prev.shape[1]
        k_ext = jnp.concatenate([k_prev.astype(k.dtype), k], axis=1)
        v_ext = jnp.concatenate([v_prev.astype(v.dtype), v], axis=1)
        attn = window_attn_cached(q, k_ext, v_ext, sink)
        new_k, new_v = k_ext[:, -Wb:], v_ext[:, -Wb:]
    y_attn = attn.astype(x.dtype) * jax.nn.silu(za)

    out = jnp.concatenate([y_conv, y_attn], axis=-1) @ w_out
    return x + gate[:, None] * out, new_conv, new_k, new_v


def setup_inputs(seed: int = 0) -> dict:
    key = jax.random.key(seed)
    ks = jax.random.split(key, 16)
    kv_buf = min(WINDOW, PAST_LEN)
    f32 = jnp.float32
    return {
        "x_prompt": jax.random.normal(ks[0], (BATCH, SEQ, D_MODEL), f32),
        "x_sample": jax.random.normal(ks[1], (DEC_BATCH, DEC_SEQ, D_MODEL), f32),
        "c_prompt": jax.random.normal(ks[2], (BATCH, D_MODEL), f32),
        "c_sample": jax.random.normal(ks[3], (DEC_BATCH, D_MODEL), f32),
        "state_conv": jax.random.normal(ks[4], (DEPTH, DEC_BATCH, CONV_K - 1, CONV_W), f32),
        "cache_k_win": jax.random.normal(ks[5], (DEPTH, DEC_BATCH, kv_buf, N_KV_HEADS, HEAD_DIM), f32),
        "cache_v_win": jax.random.normal(ks[6], (DEPTH, DEC_BATCH, kv_buf, N_KV_HEADS, HEAD_DIM), f32),
        "w_mod": jax.random.normal(ks[7], (DEPTH, D_MODEL, 3 * D_MODEL), f32) * (0.5 * D_MODEL ** -0.5),
        "b_mod": jax.random.normal(ks[8], (DEPTH, 3 * D_MODEL), f32) * 0.01,
        "norm_g": 1.0 + 0.02 * jax.random.normal(ks[9], (DEPTH, D_MODEL), f32),
        "w_in": jax.random.normal(ks[10], (DEPTH, D_MODEL, PROJ_OUT), f32) * D_MODEL ** -0.5,
        "conv_w": jax.random.normal(ks[11], (DEPTH, CONV_K, CONV_W), f32) * CONV_K ** -0.5,
        "q_norm_g": 1.0 + 0.02 * jax.random.normal(ks[12], (DEPTH, HEAD_DIM), f32),
        "k_norm_g": 1.0 + 0.02 * jax.random.normal(ks[13], (DEPTH, HEAD_DIM), f32),
        "sinks": jax.random.normal(ks[14], (DEPTH, N_HEADS), f32) * 0.5,
        "w_out": jax.random.normal(ks[15], (DEPTH, D_MIX, D_MODEL), f32) * D_MIX ** -0.5,
    }


def reference(x_prompt, x_sample, c_prompt, c_sample, state_conv, cache_k_win, cache_v_win,
              w_mod, b_mod, norm_g, w_in, conv_w, q_norm_g, k_norm_g, sinks, w_out):
    L_p = x_prompt.shape[1]
    L_s = x_sample.shape[1]
    past = PAST_LEN
    pos_p = jnp.arange(L_p, dtype=jnp.float32)
    pos_s = past + jnp.arange(L_s, dtype=jnp.float32)
    hp, hs = x_prompt, x_sample
    conv_p, k_p, v_p, conv_s, k_s, v_s = [], [], [], [], [], []
    for l in range(DEPTH):
        w = (w_mod[l], b_mod[l], norm_g[l], w_in[l], conv_w[l], q_norm_g[l], k_norm_g[l],
             sinks[l], w_out[l])
        hp, cp, kp, vp = layer(hp, c_prompt, pos_p, None, None, None, *w)
        hs, cs, ksn, vsn = layer(hs, c_sample, pos_s, state_conv[l], cache_k_win[l],
                                 cache_v_win[l], *w)
        conv_p.append(cp); k_p.append(kp); v_p.append(vp)
        conv_s.append(cs); k_s.append(ksn); v_s.append(vsn)
    return (hp, hs, jnp.stack(conv_p), jnp.stack(k_p), jnp.stack(v_p),
            jnp.stack(conv_s), jnp.stack(k_s), jnp.stack(v_s))
```

```python
import contextlib
import numpy as np
import concourse.bass as bass
import concourse.mybir as mybir
from concourse.bass_utils import run_bass_kernel_spmd

F32 = mybir.dt.float32
BF16 = mybir.dt.bfloat16
ALU = mybir.AluOpType
AF = mybir.ActivationFunctionType
AX = mybir.AxisListType

NCORES = 8
D = 1024
NBLK = 16
NB = 19
SB = 2
EPS = 1e-6
PROJ = 3328
ENGS = ("pe", "act", "dve", "pool", "sp")


class Sched:
    def __init__(self, nc, stack, nlanes_sp=12, nlanes_pool=6):
        self.nc = nc
        self.ops = {e: [] for e in ENGS}
        self.lw = {}
        self.rs = {}
        self.prog = {e: stack.enter_context(nc.semaphore("prog_" + e)) for e in ("pe", "act", "dve", "pool")}
        self.lanes = {
            "sp": [dict(sem=stack.enter_context(nc.semaphore("lsp%d" % i)), count=0, last=None) for i in range(nlanes_sp)],
            "pool": [dict(sem=stack.enter_context(nc.semaphore("lpl%d" % i)), count=0, last=None) for i in range(nlanes_pool)],
        }
        self.lane_rr = {"sp": 0, "pool": 0}
        self.ps_touch = {}
        self.all_dma = []

    ALIAS = {"tmpB": (("tq", 0), ("tq", 1), ("tq", 2), ("tq", 3)), "tB0": (("tq", 0), ("tq", 1), ("tq", 2)), "tB1": (("tq", 2), ("tq", 3)), "tmpA": (("tmpA", 0), ("tmpA", 1))}

    def _expand(self, keys):
        out = []
        for k in keys:
            if isinstance(k, str) and k in self.ALIAS:
                out.extend(self.ALIAS[k])
            else:
                out.append(k)
        return out

    def op(self, eng, fn, reads=(), writes=(), dma=False):
        reads = self._expand(reads)
        writes = self._expand(writes)
        self.nops = getattr(self, "nops", 0) + 1
        for k in list(reads) + list(writes):
            if isinstance(k, tuple) and k and k[0] == "ps":
                self.ps_touch[k[1]] = self.nops
        idx = len(self.ops[eng])
        me = (eng, idx)
        deps = {}
        for r in reads:
            w = self.lw.get(r)
            if w is not None and w != me:
                deps[w] = "raw"
            if isinstance(r, tuple) and r and r[0] == "ps":
                for rd in self.rs.get(r, ()):
                    if rd[0] != eng:
                        deps.setdefault(rd, "rar")
        for r in writes:
            w = self.lw.get(r)
            if w is not None and w != me:
                deps.setdefault(w, "waw")
            for rd in self.rs.get(r, ()):
                if rd != me:
                    deps.setdefault(rd, "war")
        rec = dict(eng=eng, fn=fn, deps=deps, dma=dma, idx=idx, marked=False)
        if dma:
            lanes = self.lanes[eng]
            ln = lanes[self.lane_rr[eng] % len(lanes)]
            self.lane_rr[eng] += 1
            if ln["last"] is not None:
                deps.setdefault(ln["last"], "raw")
            ln["count"] += 16
            rec["lane"] = ln
            rec["val"] = ln["count"]
            ln["last"] = me
            self.all_dma.append(me)
        self.ops[eng].append(rec)
        for r in reads:
            self.rs.setdefault(r, []).append(me)
        for r in writes:
            self.lw[r] = me
            self.rs[r] = []
        return me

    def _filter(self, rec):
        out = []
        for (e, j), kind in rec["deps"].items():
            src = self.ops[e][j]
            if e == rec["eng"] and not src["dma"] and not rec["dma"]:
                if e == "pe":
                    continue
                if kind == "war" and e != "pool":
                    continue
            if e == rec["eng"] and not src["dma"] and rec["dma"] and kind == "war":
                pass
            out.append((e, j))
        return out

    def finalize(self):
        for e in ENGS:
            for rec in self.ops[e]:
                fd = self._filter(rec)
                best, keep = {}, []
                for (fe, fj) in fd:
                    if self.ops[fe][fj]["dma"] or fe == "pool":
                        keep.append((fe, fj))
                    else:
                        best[fe] = max(best.get(fe, -1), fj)
                rec["fdeps"] = keep + [(fe, fj) for fe, fj in best.items()]
                for (fe, fj) in rec["fdeps"]:
                    src = self.ops[fe][fj]
                    if not src["dma"]:
                        src["marked"] = True
        for e in ENGS:
            c = 0
            for rec in self.ops[e]:
                if rec["marked"]:
                    c += 1
                    rec["cnt"] = c

    def emit_engine(self, eng, e):
        seen = {}
        for rec in self.ops[eng]:
            need = {}
            for (fe, fj) in rec["fdeps"]:
                src = self.ops[fe][fj]
                if src["dma"]:
                    sem, val = src["lane"]["sem"], src["val"]
                else:
                    sem, val = self.prog[fe], src["cnt"]
                k = id(sem)
                if k not in need or need[k][1] < val:
                    need[k] = (sem, val)
            for k, (sem, val) in need.items():
                if seen.get(k, 0) >= val:
                    continue
                e.wait_ge(sem, val)
                seen[k] = val
            inst = rec["fn"](e)
            if rec["dma"]:
                inst.then_inc(rec["lane"]["sem"], 16)
            elif rec["marked"]:
                inst.then_inc(self.prog[eng], 1)
        if eng == "sp":
            for q in ("sp", "pool"):
                for ln in self.lanes[q]:
                    if ln["count"] > 0:
                        e.wait_ge(ln["sem"], ln["count"])

    def emit(self):
        self.finalize()
        with self.nc.Block() as block:
            @block.tensor
            def _(e):
                self.emit_engine("pe", e)

            @block.scalar
            def _(e):
                self.emit_engine("act", e)

            @block.vector
            def _(e):
                self.emit_engine("dve", e)

            @block.gpsimd
            def _(e):
                self.emit_engine("pool", e)

            @block.sync
            def _(e):
                self.emit_engine("sp", e)


def _fsz(t):
    n = 1
    for s in list(t.shape)[1:]:
        n *= int(s)
    return n


def V(t, p0, np_, off, dims):
    a = t[:]
    F = a.ap[0][0]
    return bass.AP(a.tensor, p0 * F + off, [[F, np_]] + [list(d) for d in dims])


def build_nc():
    nc = bass.Bass("TRN2", target_bir_lowering=False)
    dram = {}

    def din(name, shape):
        dram[name] = nc.dram_tensor(name, list(shape), F32, kind="ExternalInput")
        return dram[name]

    def dout(name, shape):
        dram[name] = nc.dram_tensor(name, list(shape), F32, kind="ExternalOutput")
        return dram[name]

    xh = din("xh", [256, D]); xm = din("xm", [2048, D]); xs = din("xs", [128, D])
    cs = din("cs", [17, D])
    sconv = din("sconv", [2, 32, 512])
    ck = din("ck", [2, 16, 128, 128]); cv = din("cv", [2, 16, 128, 128])
    w_mod = din("w_mod", [2, D, 3 * D]); b_mod = din("b_mod", [2, 3 * D]); norm_g = din("norm_g", [2, D])
    w_in = din("w_in", [2, D, PROJ]); conv_w = din("conv_w", [2, 3, 512])
    qg = din("q_norm_g", [2, 64]); kg = din("k_norm_g", [2, 64]); sinks = din("sinks", [2, 8])
    w_out = din("w_out", [2, D, D])
    cosp = din("cosp", [128, 18 * 8]); sinp = din("sinp", [128, 18 * 16])
    coss = din("coss", [128, 8]); sins = din("sins", [128, 16])
    m_prev = din("m_prev", [128, 128]); m_cur = din("m_cur", [128, 128]); m_first = din("m_first", [128, 128])
    m_snew = din("m_snew", [128, 128]); m_scache = din("m_scache", [128, 8])
    ident_in = din("ident", [128, 128]); hvalid = din("hvalid", [128, 1])

    y_main = dout("y_main", [2048, D]); y_s = dout("y_s", [128, D])
    o_conv_p = dout("o_conv_p", [2, 2, 512]); o_k_p = dout("o_k_p", [2, 128, 128]); o_v_p = dout("o_v_p", [2, 128, 128])
    o_conv_s = dout("o_conv_s", [2, 32, 512]); o_k_s = dout("o_k_s", [2, 16, 128, 128]); o_v_s = dout("o_v_s", [2, 16, 128, 128])

    stack = contextlib.ExitStack()
    with stack:
        stack.enter_context(nc.allow_non_contiguous_dma(reason="small strided param loads"))
        try:
            stack.enter_context(nc.allow_low_precision(reason="bf16 matmul operands, fp32 accumulate"))
        except Exception:
            pass
        S = Sched(nc, stack)

        def sb(name, shape, dt=F32):
            return stack.enter_context(nc.sbuf_tensor(name, list(shape), dt))

        X = sb("X", [128, NB - 1, D])
        Win = sb("Win", [128, 8, PROJ], BF16)
        Wout = sb("Wout", [128, 8, D], BF16)
        gate_p = sb("gate_p", [128, D])
        modT = [sb("modT%d" % l, [128, 24, 17]) for l in range(2)]
        tmpA = sb("tmpA", [128, D]); tmpB = sb("tmpB", [128, D])
        hs = sb("hs", [128, D], BF16)
        hT = sb("hT", [128, 3, 8, 264], BF16)
        yT = sb("yT", [128, 8, SB * 128], BF16)
        qk = sb("qk", [128, 640]); qn = sb("qn", [128, 640], BF16)
        qT = sb("qT", [128, 4, 128], BF16)
        kT = [sb("kT%d" % i, [128, 128], BF16) for i in range(2)]
        Vaug = [sb("Vaug%d" % i, [128, 2, 66], BF16) for i in range(3)]
        vout = sb("vout", [128, 128])
        szb = [sb("sz%d" % i, [128, 512]) for i in range(2)]
        PT = sb("PT", [128, 4, 512], BF16)
        yattn = sb("yattn", [128, 512], BF16)
        uT = [sb("uT%d" % c, [128, 2 + SB * 128]) for c in range(4)]
        ring = sb("ring", [128, 2, 8, 256], BF16)
        cosT = sb("cosT", [128, 18, 8]); sinT = sb("sinT", [128, 18, 16])
        cosS = sb("cosS", [128, 8]); sinS = sb("sinS", [128, 16])
        mprev = sb("mprev", [128, 128], BF16); mcur = sb("mcur", [128, 128], BF16)
        mfirst = sb("mfirst", [128, 128], BF16); msnew = sb("msnew", [128, 128], BF16)
        mscache = sb("mscache", [128, 8], BF16)
        identF = sb("identF", [128, 128]); identB = sb("identB", [128, 128], BF16)
        gqkL = [sb("gqk%d" % l, [128, 128]) for l in range(2)]
        parT = sb("parT", [128, 2, 44])
        sc17T = sb("sc17T", [128, 8, 17], BF16)
        small = sb("small", [128, 64])
        consts = sb("consts", [128, 16])
        sk2 = sb("sk2", [128, 2, 8])
        PTf = PT[:].bitcast(F32)
        PF = PTf.ap[0][0]

        class _UTS:
            def __init__(self, cc):
                self.cc = cc
            def v(self, t0, n):
                return bass.AP(PTf.tensor, self.cc * 160 + t0, [[PF, 128], [10, 16], [1, n]])
        uTs = [_UTS(c) for c in range(4)]
        UTSK = [("PT", kb, g) for kb in range(2) for g in range(2)]
        parS = V(tmpA, 0, 44, 640, [[128, 2], [1, 128]])
        ust = V(tmpB, 0, 128, 896, [[32, 4], [1, 32]])
        sconvS = tmpA[0:32, 0:512]
        cvo = tmpB[0:32, 512:1024]
        ptq_t = sb("ptq_t", [128, 256])
        hval = small[:, 3:4]
        negM = small[:, 4:6]
        pso = [stack.enter_context(nc.psum_tensor("ps%d" % i, [128, 512], F32)) for i in range(8)]

        rot = {"i": 0, "t": 0}

        def bank():
            k = min((0, 2, 3, 4, 5, 6, 7), key=lambda b: S.ps_touch.get(b, 0))
            S.nops = getattr(S, "nops", 0) + 1
            S.ps_touch[k] = S.nops
            return k

        def tbank():
            return 0

        def P(k, p0, np_, off, dims):
            return V(pso[k], p0, np_, off, dims)

        def PB(k, p0, np_, off, dims):
            a = pso[k][:].bitcast(BF16)
            F = a.ap[0][0]
            return bass.AP(a.tensor, p0 * F + off, [[F, np_]] + [list(d) for d in dims])

        def dma(q, out, in_, reads, writes):
            return S.op(q, lambda e, o=out, i=in_: e.dma_start(out=o, in_=i), reads, writes, dma=True)

        def act(out, in_, func, reads, writes, **kw):
            return S.op("act", lambda e: e.activation(out=out, in_=in_, func=func, **kw), reads, writes)

        def tt(eng, out, in0, in1, op, reads, writes):
            return S.op(eng, lambda e: e.tensor_tensor(out=out, in0=in0, in1=in1, op=op), reads, writes)

        def ts(eng, out, in0, s1, s2, op0, op1, reads, writes):
            return S.op(eng, lambda e: e.tensor_scalar(out=out, in0=in0, scalar1=s1, scalar2=s2, op0=op0, op1=op1), reads, writes)

        def stt(eng, out, in0, sc, in1, op0, op1, reads, writes):
            return S.op(eng, lambda e: e.scalar_tensor_tensor(out=out, in0=in0, scalar=sc, in1=in1, op0=op0, op1=op1), reads, writes)

        def cp(eng, out, in_, reads, writes):
            if eng == "act":
                return act(out, in_, AF.Copy, reads, writes)
            return S.op(eng, lambda e: e.tensor_copy(out=out, in_=in_), reads, writes)

        def mm(out, lhsT, rhs, start, stop, reads, writes):
            return S.op("pe", lambda e: e.matmul(out, lhsT, rhs, start=start, stop=stop), reads, writes)

        def tr(out, in_, ident, reads, writes):
            return S.op("pe", lambda e: e.transpose(out, in_, ident), reads, writes)

        def mset(eng, ap, val, writes):
            return S.op(eng, lambda e: e.memset(ap, val), (), writes)

        def xap(blk):
            if blk == 0:
                return X[:, 5, :]
            return X[:, blk - 1, :]

        def xkey(blk):
            return ("X", 6) if blk == 0 else ("X", blk)

        xh_ap = xh.ap(); xm_ap = xm.ap()
        dma("sp", xap(0), xh_ap[0:128, :], (), [xkey(0)])
        dma("sp", xap(1), xh_ap[128:256, :], (), [xkey(1)])
        dma("sp", identF[:, :], ident_in.ap(), (), ["identF"])
        cp("dve", identB[:, :], identF[:, :], ["identF"], ["identB"])
        cs17 = X[0:17, 4, :]
        csb = hs[0:17, :]
        dma("sp", cs17, cs.ap(), (), [("X", 5)])
        for l in range(2):
            dma("sp", V(tmpA, 0, 24, 640 + l * 128, [[1, 128]]), b_mod.ap()[l].rearrange("(j p) -> j p", p=128), (), [("parS", l, 0)])
            dma("sp", V(tmpA, 24, 8, 640 + l * 128, [[1, 128]]), norm_g.ap()[l].rearrange("(c p) -> c p", p=128), (), [("parS", l, 1)])
            dma("sp", V(tmpA, 32, 12, 640 + l * 128, [[1, 128]]), conv_w.ap()[l].rearrange("i (c p) -> (i c) p", p=128), (), [("parS", l, 2)])
            dma("sp", gqkL[l][:, 0:64], bass.AP(qg.ap().tensor, l * 64, [[0, 128], [1, 64]]), (), [("gqk", l, 0)])
            dma("sp", gqkL[l][:, 64:128], bass.AP(kg.ap().tensor, l * 64, [[0, 128], [1, 64]]), (), [("gqk", l, 1)])
            dma("sp", sk2[:, l, :], bass.AP(sinks.ap().tensor, l * 8, [[0, 128], [1, 8]]), (), [("sk2", l)])
        for b in range(2, 5):
            dma("sp", xap(b), xm_ap[(b - 2) * 128:(b - 1) * 128, :], (), [xkey(b)])
        for k, (dst, src, nm) in enumerate(((mprev, m_prev, "mprev"), (mcur, m_cur, "mcur"), (mfirst, m_first, "mfirst"), (msnew, m_snew, "msnew"))):
            dma("sp", tmpA[:, k * 128:(k + 1) * 128], src.ap(), (), [("mk", k)])
            cp("dve", dst[:, :], tmpA[:, k * 128:(k + 1) * 128], [("mk", k), "tmpA"], [nm])
        dma("sp", tmpA[:, 512:520], m_scache.ap(), (), [("mk", 4)])
        cp("dve", mscache[:, :], tmpA[:, 512:520], [("mk", 4), "tmpA"], ["mscache"])
        dma("sp", cosT[:, :, :], cosp.ap().rearrange("p (b j) -> p b j", j=8), (), ["cosT"])
        dma("sp", sinT[:, :, :], sinp.ap().rearrange("p (b j) -> p b j", j=16), (), ["sinT"])
        dma("sp", cosS[:, :], coss.ap(), (), ["cosS"])
        dma("sp", sinS[:, :], sins.ap(), (), ["sinS"])
        dma("sp", small[:, 3:4], hvalid.ap(), (), ["hval"])
        mset("dve", consts[:, 0:10], -0.5, ["consts"])
        mset("dve", consts[:, 10:11], 0.5, ["consts"])
        for i in range(3):
            mset("pool", Vaug[i][:, :, 64:66], 2.0, [("Vaug", i)])
        for c in range(4):
            mset("pool", uT[c][:, :], 0.0, [("uT", c)])
        for l in range(2):
            bk = bank()
            tr(P(bk, 0, 128, 0, [[1, 44]]), V(tmpA, 0, 44, 640 + l * 128, [[1, 128]]), identF[0:44, 0:44], [("parS", l, 0), ("parS", l, 1), ("parS", l, 2), "identF"], [("ps", bk)])
            cp("act", parT[:, l, :], P(bk, 0, 128, 0, [[1, 44]]), [("ps", bk), "tmpA"], [("parT", l)])

        act(tmpB[0:17, :], cs17, AF.Tanh, [("X", 5)], ["tmpB"], scale=0.5)
        stt("dve", cs17, tmpB[0:17, :], 1.0, cs17, ALU.add, ALU.mult, ["tmpB", ("X", 5)], [("X", 5)])
        ts("dve", csb, cs17, 0.5, None, ALU.mult, ALU.bypass, [("X", 5)], ["hs"])
        tb = tbank()
        for c in range(8):
            tr(PB(tb, 0, 128, c * 32, [[1, 17]]), hs[0:17, c * 128:(c + 1) * 128], identB[0:17, 0:17], ["hs", "identB"], [("ps", tb)])
        cp("act", sc17T[:, :, :], PB(tb, 0, 128, 0, [[32, 8], [1, 17]]), [("ps", tb)], ["sc17T"])

        def load_w_piece(l, name):
            wl = w_in.ap()[l]
            if name == "q":
                for i in range(4):
                    for g in range(2):
                        h = g * 4 + i
                        src = wl[:, 2048 + h * 64:2048 + (h + 1) * 64].rearrange("(c p) n -> p c n", p=128)
                        c0 = 2048 + i * 128 + g * 64
                        dma("pool", Win[:, :, c0:c0 + 64], src, (), [("W", "q", h)])
            elif name == "kv":
                src = wl[:, 2560:2816].rearrange("(c p) n -> p c n", p=128)
                dma("pool", Win[:, :, 2560:2816], src, (), [("W", "kv")])
            elif name == "za":
                src = wl[:, 2816:3328].rearrange("(c p) n -> p c n", p=128)
                dma("pool", Win[:, :, 2816:3328], src, (), [("W", "za")])
            elif name.startswith("cv"):
                qi = int(name[2])
                src = wl[:, qi * 512:(qi + 1) * 512].rearrange("(c p) n -> p c n", p=128)
                dma("pool", Win[:, :, qi * 512:(qi + 1) * 512], src, (), [("W", name)])
            elif name == "out":
                for hf in range(2):
                    src = w_out.ap()[l][:, hf * 512:(hf + 1) * 512].rearrange("(c p) n -> p c n", p=128)
                    dma("pool", Wout[:, :, hf * 512:(hf + 1) * 512], src, (), [("W", "out", hf)])

        Xbf = X[:].bitcast(BF16)
        XF = Xbf.ap[0][0]
        WM0 = 6 * 2048
        WMK = [("X", b) for b in range(7, 19)]

        def mod0_dma(s6):
            src = w_mod.ap()[0][:, s6 * 512:(s6 + 1) * 512].rearrange("(c p) n -> p c n", p=128)
            dst = bass.AP(Xbf.tensor, WM0 + s6 * 512, [[XF, 128], [3072, 8], [1, 512]])
            dma("pool", dst, src, (), [("wm0", s6)])

        def mod0_mm(s):
            bk = bank()
            for jj in range(2):
                j = 2 * s + jj
                for c in range(8):
                    lhs = bass.AP(Xbf.tensor, WM0 + c * 3072 + j * 128, [[XF, 128], [1, 128]])
                    mm(P(bk, 0, 128, jj * 17, [[1, 17]]), lhs, sc17T[:, c, :], c == 0, c == 7, WMK + [("wm0", s // 2), "sc17T"], [("ps", bk)])
            j0 = 2 * s
            tt("dve", modT[0][:, j0:j0 + 2, :], P(bk, 0, 128, 0, [[17, 2], [1, 17]]),
               V(parT, 0, 128, j0, [[1, 2], [0, 17]]), ALU.add, [("ps", bk), ("parT", 0)], [("modT", 0, s)])

        def mod_dma(l, s):
            slot = s % 2
            src = w_mod.ap()[l][:, s * 256:(s + 1) * 256].rearrange("(c p) n -> p c n", p=128)
            dma("pool", ring[:, slot, :, :], src, (), [("ring", slot)])

        def mod_mm(l, s):
            slot = s % 2
            bk = bank()
            for jj in range(2):
                for c in range(8):
                    mm(P(bk, 0, 128, jj * 17, [[1, 17]]), ring[:, slot, c, jj * 128:(jj + 1) * 128], sc17T[:, c, :],
                       c == 0, c == 7, [("ring", slot), "sc17T"], [("ps", bk)])
            j0 = 2 * s
            tt("dve", modT[l][:, j0:j0 + 2, :], P(bk, 0, 128, 0, [[17, 2], [1, 17]]),
               V(parT, 0, 128, l * 44 + j0, [[1, 2], [0, 17]]), ALU.add, [("ps", bk), ("parT", l)], [("modT", l, s)])

        def layer_params(l):
            gqk = gqkL[l]
            gk_ = [("gqk", l, 0), ("gqk", l, 1)]
            tt("dve", tmpB[:, 0:64], gqk[:, 0:64], gqk[:, 0:64], ALU.mult, gk_, ["tmpB"])
            S.op("dve", lambda e: e.tensor_reduce(out=small[:, 40:41], in_=tmpB[:, 0:64], axis=AX.X, op=ALU.max), ["tmpB"], ["small_m"])
            tt("dve", tmpB[:, 0:64], gqk[:, 64:128], gqk[:, 64:128], ALU.mult, gk_ + ["small_m"], ["tmpB"])
            S.op("dve", lambda e: e.tensor_reduce(out=small[:, 41:42], in_=tmpB[:, 0:64], axis=AX.X, op=ALU.max), ["tmpB"], ["small_m2"])
            tt("dve", small[:, 42:43], small[:, 40:41], small[:, 41:42], ALU.mult, ["small_m", "small_m2"], ["small_m3"])
            tt("pool", small[:, 43:44], small[:, 42:43], consts[:, 10:11], ALU.pow, ["small_m3", "consts"], ["small_m4"])
            ts("dve", small[:, 4 + l:5 + l], small[:, 43:44], -8.0, None, ALU.mult, ALU.bypass, ["small_m4"], [("negM", l)])
            act(sk2[:, l, :], sk2[:, l, :], AF.Exp, [("sk2", l), ("negM", l)], [("sk2", l)], bias=small[:, 4 + l:5 + l], scale=1.0)
            ts("dve", sk2[:, l, :], sk2[:, l, :], 2.0, None, ALU.mult, ALU.bypass, [("sk2", l)], [("sk2", l)])

        def finish_mod(l):
            mk = [("modT", l, s) for s in range(4, 8)]
            stt("dve", modT[l][:, 8:16, :], modT[l][:, 8:16, :], 1.0, V(parT, 0, 128, l * 44 + 24, [[1, 8], [0, 17]]), ALU.add, ALU.mult,
                mk + [("parT", l)], [("modT", l, "gs")])

        def build_gate(l, sample):
            mk = [("modT", l, s) for s in range(12)]
            if sample:
                cp("dve", V(tmpB, 0, 128, 0, [[128, 8], [8, 16], [1, 8]]), V(modT[l], 0, 128, 16 * 17 + 1, [[17, 8], [1, 16], [0, 8]]), mk, ["tmpB"])
            dst, nm = (tmpA, "tmpA") if sample else (gate_p, "gate_p")
            for hf in range(2):
                bk = bank()
                for c4 in range(4):
                    c = hf * 4 + c4
                    if not sample:
                        lhs = V(modT[l], 0, 128, (16 + c) * 17, [[0, 128]])
                        rk = mk
                    else:
                        lhs = tmpB[:, c * 128:(c + 1) * 128]
                        rk = ["tmpB"]
                    mm(P(bk, 0, 128, c4 * 128, [[1, 128]]), lhs, identF[:, :], True, True, rk + ["identF"], [("ps", bk)])
                cp("act", dst[:, hf * 512:(hf + 1) * 512], P(bk, 0, 128, 0, [[1, 512]]), [("ps", bk)], [nm])

        state = {"l": 0, "ulast": 0}
        ROPE0 = 640

        def HK(pair, j):
            return [("hT", pair, j, c) for c in range(8)]

        def hT_ap(slot, c, j0, n):
            return V(hT, 0, 128, slot * 8 * 264 + c * 264 + j0 * 128, [[1, n]])

        def stats_a(l, blk, par):
            xa = xap(blk)
            c0 = 6 * par
            act(V(small, 0, 128, 44, [[0, D]]), xa, AF.Square, [xkey(blk)], ["junk", ("ss", par)], accum_out=small[:, c0:c0 + 1])
            ts("dve", small[:, c0 + 1:c0 + 2], small[:, c0:c0 + 1], 1.0 / D, EPS, ALU.mult, ALU.add, [("ss", par)], [("ms", par)])
            tt("pool", small[:, c0 + 2:c0 + 3], small[:, c0 + 1:c0 + 2], consts[:, 0:1], ALU.pow, [("ms", par), "consts"], [("rstd", par)])

        def stats_b(l, blk, par):
            c0 = 6 * par
            act(hs[:, :], xap(blk), AF.Identity, [xkey(blk), ("rstd", par)], ["hs"], scale=small[:, c0 + 2:c0 + 3])

        def stats(l, blk):
            stats_a(l, blk, 0)
            stats_b(l, blk, 0)

        def a_tr(l, pair, j, sample, tb=1):
            for c in range(8):
                tr(PB(tb, 0, 128, c * 128, [[1, 128]]), hs[:, c * 128:(c + 1) * 128], identB[:, :], ["hs", "identB"], [("ps", tb)])
            mkeys = [("modT", l, s) for s in range(4)] + [("modT", l, "gs")]
            hk = HK(pair, j)
            if not sample:
                for c in range(8):
                    act(hT_ap(pair, c, j, 128), PB(tb, 0, 128, c * 128, [[1, 128]]), AF.Identity, [("ps", tb)] + mkeys, [hk[c]],
                        scale=modT[l][:, 8 + c, 0:1], bias=modT[l][:, c, 0:1])
            else:
                tt("dve", V(tmpB, 0, 128, 0, [[128, 8], [8, 16], [1, 8]]), PB(tb, 0, 128, 0, [[128, 8], [8, 16], [1, 8]]),
                   V(modT[l], 0, 128, 8 * 17 + 1, [[17, 8], [1, 16], [0, 8]]), ALU.mult, [("ps", tb)] + mkeys, ["tmpB"])
                tt("dve", V(hT, 0, 128, pair * 8 * 264 + j * 128, [[264, 8], [8, 16], [1, 8]]), V(tmpB, 0, 128, 0, [[128, 8], [8, 16], [1, 8]]),
                   V(modT[l], 0, 128, 1, [[17, 8], [1, 16], [0, 8]]), ALU.add, ["tmpB"] + mkeys, hk)

        def b_mm(l, blk, pair, j, r3, zi, sample, kv_only, want_v32):
            hk = HK(pair, j)
            if not kv_only:
                bq = bank()
                for c in range(8):
                    mm(P(bq, 0, 128, 0, [[1, 512]]), hT_ap(pair, c, j, 128), Win[:, c, 2048:2560], c == 0, c == 7, hk + [("W", "q", hh) for hh in range(8)], [("ps", bq)])
            bkv = bank()
            for c in range(8):
                mm(P(bkv, 0, 128, 0, [[1, 256]]), hT_ap(pair, c, j, 128), Win[:, c, 2560:2816], c == 0, c == 7, hk + [("W", "kv")], [("ps", bkv)])
            if not kv_only:
                bza = bank()
                for c in range(8):
                    mm(P(bza, 0, 128, 0, [[1, 512]]), hT_ap(pair, c, j, 128), Win[:, c, 2816:3328], c == 0, c == 7, hk + [("W", "za")], [("ps", bza)])
                cp("act", qk[:, 0:512], P(bq, 0, 128, 0, [[1, 512]]), [("ps", bq)], ["qk"])
            cp("act", qk[:, 512:640], P(bkv, 0, 128, 0, [[1, 128]]), [("ps", bkv)], ["qk"])
            cp("act", Vaug[r3][:, :, 0:64], P(bkv, 0, 128, 128, [[64, 2], [1, 64]]), [("ps", bkv)], [("Vaug", r3)])
            if want_v32:
                cp("act", vout[:, :], P(bkv, 0, 128, 128, [[1, 128]]), [("ps", bkv)], ["vout"])
            h0, nh = (8, 2) if kv_only else (0, 10)
            c0 = h0 * 64
            ncol = nh * 64
            if not kv_only:
                act(szb[zi][:, :], P(bza, 0, 128, 0, [[1, 512]]), AF.Tanh, [("ps", bza)], [("sz", zi)], scale=0.5)
                stt("dve", szb[zi][:, :], szb[zi][:, :], 1.0, P(bza, 0, 128, 0, [[1, 512]]), ALU.add, ALU.mult, [("sz", zi), ("ps", bza)], [("sz", zi)])
            tt("dve", tmpB[:, c0:c0 + ncol], qk[:, c0:c0 + ncol], qk[:, c0:c0 + ncol], ALU.mult, ["qk"], ["tB0"])
            S.op("dve", lambda e: e.tensor_reduce(out=small[:, 10 + h0:10 + h0 + nh], in_=V(tmpB, 0, 128, c0, [[64, nh], [1, 64]]),
                                                  axis=AX.X, op=ALU.add), ["tB0"], ["ssq"])
            ts("dve", small[:, 20 + h0:20 + h0 + nh], small[:, 10 + h0:10 + h0 + nh], 1.0 / 64, EPS, ALU.mult, ALU.add, ["ssq"], ["msq"])
            tt("pool", small[:, 30 + h0:30 + h0 + nh], small[:, 20 + h0:20 + h0 + nh], consts[:, h0:h0 + nh], ALU.pow, ["msq", "consts"], ["rq"])
            if not kv_only:
                tt("dve", V(qk, 0, 128, 0, [[64, 8], [1, 64]]), V(qk, 0, 128, 0, [[64, 8], [1, 64]]), V(gqkL[l], 0, 128, 0, [[0, 8], [1, 64]]),
                   ALU.mult, ["qk", ("gqk", l, 0)], ["qk"])
            tt("dve", V(qk, 0, 128, 512, [[64, 2], [1, 64]]), V(qk, 0, 128, 512, [[64, 2], [1, 64]]), V(gqkL[l], 0, 128, 64, [[0, 2], [1, 64]]),
               ALU.mult, ["qk", ("gqk", l, 1)], ["qk"])
            tt("dve", V(qk, 0, 128, c0, [[64, nh], [1, 64]]), V(qk, 0, 128, c0, [[64, nh], [1, 64]]),
               V(small, 0, 128, 30 + h0, [[1, nh], [0, 64]]), ALU.mult, ["qk", "rq"], ["qk"])
            if sample:
                cosb = V(cosS, 0, 128, 0, [[0, nh], [0, 2], [1, 8]]); sinb = V(sinS, 0, 128, 0, [[0, nh], [1, 16]])
                tk = ["cosS", "sinS"]
            else:
                cosb = V(cosT, 0, 128, blk * 8, [[0, nh], [0, 2], [1, 8]]); sinb = V(sinT, 0, 128, blk * 16, [[0, nh], [1, 16]])
                tk = ["cosT", "sinT"]
            xr = V(qk, 0, 128, c0, [[64, nh], [8, 2], [1, 8]])
            xsw = V(qk, 0, 128, c0 + 8, [[64, nh], [-8, 2], [1, 8]])
            t1 = V(tmpB, 0, 128, ROPE0, [[16, nh], [8, 2], [1, 8]])
            t2 = V(tmpB, 0, 128, ROPE0 + 160, [[16, nh], [1, 16]])
            tt("dve", t1, xr, cosb, ALU.mult, ["qk"] + tk, ["tB1"])
            sin3 = V(sinS if sample else sinT, 0, 128, 0 if sample else blk * 16, [[0, nh], [8, 2], [1, 8]])
            tt("dve", V(tmpB, 0, 128, ROPE0 + 160, [[16, nh], [8, 2], [1, 8]]), xsw, sin3, ALU.mult, ["qk"] + tk, ["tB1"])
            tt("dve", V(qk, 0, 128, c0, [[64, nh], [1, 16]]), V(tmpB, 0, 128, ROPE0, [[16, nh], [1, 16]]), t2, ALU.add, ["tB1"], ["qk"])
            cp("dve", qn[:, c0:c0 + ncol], qk[:, c0:c0 + ncol], ["qk"], ["qn"])

        def qk_tr(r3, kv_only):
            tb = bank()
            if not kv_only:
                for i in range(4):
                    tr(PB(tb, 0, 128, i * 128, [[1, 128]]), qn[:, i * 128:(i + 1) * 128], identB[:, :], ["qn", "identB"], [("ps", tb)])
            tr(PB(tb, 0, 128, 512, [[1, 128]]), qn[:, 512:640], identB[:, :], ["qn", "identB"], [("ps", tb)])
            if not kv_only:
                cp("act", qT[:, :, :], PB(tb, 0, 128, 0, [[128, 4], [1, 128]]), [("ps", tb)], ["qT"])
            cp("act", kT[r3][:, :], PB(tb, 0, 128, 512, [[1, 128]]), [("ps", tb)], [("kT", r3)])

        def exp_mask(bk, kb, g, maskap, mkey, c0=0, n=512):
            l = state["l"]
            act(PT[:, kb * 2 + g, c0:c0 + n], P(bk, 0, 128, c0, [[1, n]]), AF.Exp, [("ps", bk), ("negM", l)], [("PT", kb, g)],
                bias=small[:, 4 + l:5 + l], scale=0.125)
            tt("dve", PT[:, kb * 2 + g, c0:c0 + n], PT[:, kb * 2 + g, c0:c0 + n], maskap, ALU.mult, [("PT", kb, g), mkey], [("PT", kb, g)])

        def s_exp(l, r_prev, r_cur, first_main):
            state["l"] = l
            sbk = {}
            for g in range(2):
                for kb, kr in ((0, r_prev), (1, r_cur)):
                    bk = bank()
                    sbk[(kb, g)] = bk
                    mm(P(bk, 0, 128, 0, [[1, 512]]), kT[kr][g * 64:(g + 1) * 64, :], V(qT, g * 64, 64, 0, [[1, 512]]), True, True,
                       [("kT", kr), "qT"], [("ps", bk)])
            for g in range(2):
                mp = (mfirst, "mfirst") if first_main else (mprev, "mprev")
                exp_mask(sbk[(0, g)], 0, g, V(mp[0], 0, 128, 0, [[0, 4], [1, 128]]), mp[1])
                exp_mask(sbk[(1, g)], 1, g, V(mcur, 0, 128, 0, [[0, 4], [1, 128]]), "mcur")

        def normalize(l, osrc, dsrc, okeys, zi):
            for g in range(2):
                tt("dve", small[:, 48 + g * 4:52 + g * 4], dsrc[g], sk2[:, l, g * 4:(g + 1) * 4], ALU.add, okeys[g] + [("sk2", l)], [("den", g)])
            S.op("dve", lambda e: e.reciprocal(out=small[:, 56:64], in_=small[:, 48:56]), [("den", 0), ("den", 1)], [("rden", 0), ("rden", 1)])
            tt("dve", V(szb[zi], 0, 128, 0, [[64, 8], [1, 64]]), V(szb[zi], 0, 128, 0, [[64, 8], [1, 64]]), V(small, 0, 128, 56, [[1, 8], [0, 64]]),
               ALU.mult, [("sz", zi), ("rden", 0), ("rden", 1)], [("sz", zi)])
            for g in range(2):
                tt("dve", V(yattn, 0, 128, g * 256, [[64, 4], [1, 64]]), osrc[g], V(szb[zi], 0, 128, g * 256, [[64, 4], [1, 64]]), ALU.mult,
                   okeys[g] + [("sz", zi)], ["yattn"])

        def pv_norm(l, r_prev, r_cur, zi):
            ob = [bank(), bank()]
            for g in range(2):
                for i in range(4):
                    o = P(ob[g], 0, 128, i * 65, [[1, 65]])
                    mm(o, PT[:, 0 * 2 + g, i * 128:(i + 1) * 128], Vaug[r_prev][:, g, 0:65], True, False,
                       [("PT", 0, g), ("Vaug", r_prev)], [("ps", ob[g])])
                    mm(o, PT[:, 1 * 2 + g, i * 128:(i + 1) * 128], Vaug[r_cur][:, g, 0:65], False, True,
                       [("PT", 1, g), ("Vaug", r_cur)], [("ps", ob[g])])
            osrc = [P(ob[g], 0, 128, 0, [[65, 4], [1, 64]]) for g in range(2)]
            dsrc = [P(ob[g], 0, 128, 64, [[65, 4]]) for g in range(2)]
            normalize(l, osrc, dsrc, [[("ps", ob[0])], [("ps", ob[1])]], zi)

        def y_tr(j):
            tb = bank()
            for pr in range(4):
                tr(PB(tb, 0, 128, pr * 128, [[1, 128]]), yattn[:, pr * 128:(pr + 1) * 128], identB[:, :], ["yattn", "identB"], [("ps", tb)])
            cp("act", V(yT, 0, 128, 4 * SB * 128 + j * 128, [[SB * 128, 4], [1, 128]]), PB(tb, 0, 128, 0, [[128, 4], [1, 128]]),
               [("ps", tb)], [("yTa", j)])

        def conv(l, pair, nblk, sample, u_only=False, halo_to_main=False):
            N = nblk * 128
            hk = [k for j in range(nblk) for k in HK(pair, j)]
            pend = None
            for cc in range(4):
                p = cc % 2
                HC = 512 + p * 256
                hck = ("tq", 2 + p)
                cak = ("tq", p)
                bkA = bank()
                bkB = None if u_only else bank()
                loc = {1: (bkA, 0), 2: (bkA, 256), 3: (bkB, 0), 0: (bkB, 256)}
                for qi in ((1, 2) if u_only else (1, 2, 3, 0)):
                    bk, o0 = loc[qi]
                    for c in range(8):
                        mm(P(bk, 0, 128, o0, [[1, N]]), Win[:, c, qi * 512 + cc * 128:qi * 512 + (cc + 1) * 128], hT_ap(pair, c, 0, N), c == 0, c == 7,
                           hk + [("W", "cv%d" % qi)], [("ps", bk)])
                cp("act", tmpB[:, HC:HC + N], P(bkA, 0, 128, 256, [[1, N]]), [("ps", bkA)], [hck])
                if not sample:
                    ukey = ("uT", cc)
                    if halo_to_main:
                        ts("dve", uT[cc][:, 0:2], uT[cc][:, state["ulast"]:state["ulast"] + 2], small[:, 3:4], None, ALU.mult, ALU.bypass,
                           [ukey, "hval"], [ukey])
                    else:
                        cp("dve", uT[cc][:, 0:2], uT[cc][:, state["ulast"]:state["ulast"] + 2], [ukey], [ukey])
                    tt("dve", uT[cc][:, 2:2 + N], P(bkA, 0, 128, 0, [[1, N]]), tmpB[:, HC:HC + N], ALU.mult, [("ps", bkA), hck, ukey], [ukey])
                    u0 = uT[cc][:, 0:N]; u1 = uT[cc][:, 1:1 + N]; u2 = uT[cc][:, 2:2 + N]
                    shp = [[1, N]]
                    urk = [ukey]
                else:
                    tt("dve", uTs[cc].v(2, 8), P(bkA, 0, 128, 0, [[8, 16], [1, 8]]), V(tmpB, 0, 128, HC, [[8, 16], [1, 8]]), ALU.mult,
                       [("ps", bkA), hck] + UTSK, UTSK)
                    u0 = uTs[cc].v(0, 8); u1 = uTs[cc].v(1, 8); u2 = uTs[cc].v(2, 8)
                    shp = [[8, 16], [1, 8]]
                    urk = UTSK
                if u_only:
                    continue
                ca = V(tmpB, 0, 128, p * 256, shp)
                tq = V(ptq_t, 0, 128, 0, shp)
                w0 = parT[:, l, 32 + cc:32 + cc + 1]; w1 = parT[:, l, 36 + cc:36 + cc + 1]; w2 = parT[:, l, 40 + cc:40 + cc + 1]
                ts("pool", ca, u0, w0, 0.0, ALU.mult, ALU.add, urk + [("parT", l)], [cak])
                ts("pool", tq, u1, w1, 0.0, ALU.mult, ALU.add, urk + [("parT", l)], ["ptq"])
                tt("pool", ca, ca, tq, ALU.add, [cak, "ptq"], [cak])
                ts("pool", tq, u2, w2, 0.0, ALU.mult, ALU.add, urk + [("parT", l)], ["ptq"])
                tt("pool", ca, ca, tq, ALU.add, [cak, "ptq"], [cak])
                act(tmpB[:, HC:HC + N], P(bkB, 0, 128, 0, [[1, N]]), AF.Tanh, [("ps", bkB)], [hck], scale=0.5)
                stt("dve", tmpB[:, HC:HC + N], tmpB[:, HC:HC + N], 1.0, P(bkB, 0, 128, 0, [[1, N]]), ALU.add, ALU.mult, [hck, ("ps", bkB)], [hck])
                tt("dve", tmpB[:, HC:HC + N], tmpB[:, HC:HC + N], P(bkB, 0, 128, 256, [[1, N]]), ALU.mult, [hck, ("ps", bkB)], [hck])
                if pend is not None:
                    pend()
                def _fin(cc=cc, p=p, HC=HC, cak=cak, hck=hck, ca=ca, u2=u2, w2=w2, urk=urk):
                    stt("dve", yT[:, cc, 0:N], V(tmpB, 0, 128, p * 256, [[1, N]]), 0.5, tmpB[:, HC:HC + N], ALU.mult, ALU.mult, [cak, hck], [("yTc", cc)])
                pend = _fin
            if pend is not None:
                pend()
            if not sample:
                state["ulast"] = N

        def e_proj(l, blk, j, sample):
            yk = [("yTc", cc) for cc in range(4)] + [("yTa", j)]
            for hf in range(2):
                bk = bank()
                for kc in range(8):
                    mm(P(bk, 0, 128, 0, [[1, 512]]), yT[:, kc, j * 128:(j + 1) * 128], Wout[:, kc, hf * 512:(hf + 1) * 512], kc == 0, kc == 7,
                       yk + [("W", "out", hf)], [("ps", bk)])
                xv = X[:, blk - 1, hf * 512:(hf + 1) * 512]
                if not sample:
                    tt("dve", tmpA[:, hf * 512:(hf + 1) * 512], P(bk, 0, 128, 0, [[1, 512]]), gate_p[:, hf * 512:(hf + 1) * 512], ALU.mult,
                       [("ps", bk), "gate_p"], [("tmpA", hf)])
                    tt("pool", xv, tmpA[:, hf * 512:(hf + 1) * 512], xv, ALU.add, [("tmpA", hf), xkey(blk)], [xkey(blk)])
                else:
                    tt("dve", P(bk, 0, 128, 0, [[1, 512]]), P(bk, 0, 128, 0, [[1, 512]]), tmpA[:, hf * 512:(hf + 1) * 512], ALU.mult,
                       [("ps", bk), "tmpA"], [("ps", bk)])
                    tt("dve", xv, P(bk, 0, 128, 0, [[1, 512]]), xv, ALU.add, [("ps", bk), xkey(blk)], [xkey(blk)])
            if l == 1:
                if sample:
                    dma("sp", y_s.ap(), X[:, blk - 1, :], [xkey(blk)], ())
                elif blk >= 2:
                    dma("sp", y_main.ap()[(blk - 2) * 128:(blk - 1) * 128, :], X[:, blk - 1, :], [xkey(blk)], ())

        tmpAb = tmpA[:].bitcast(BF16)
        TAF = tmpAb.ap[0][0]
        VCK = ["tmpA"]

        def Vc_ap(b, g, n):
            return bass.AP(tmpAb.tensor, (b * 2 + g) * 64, [[TAF, 128], [1, n]])

        def sample_cache_k_dma(l):
            dma("pool", V(ring, 0, 128, 0, [[128, 16], [1, 128]]), ck.ap()[l].rearrange("b k d -> k b d"), (), [("ring", 0)])
            dma("sp", o_k_s.ap()[l][:, 0:120, :], ck.ap()[l][:, 8:128, :], (), ())
            dma("sp", o_v_s.ap()[l][:, 0:120, :], cv.ap()[l][:, 8:128, :], (), ())

        def sample_cache_k_tr(l):
            for b in range(16):
                if b % 8 == 0:
                    tb = bank()
                tr(PB(tb, 0, 128, (b % 8) * 128, [[1, 128]]), V(ring, 0, 128, b * 128, [[1, 128]]), identB[:, :], [("ring", 0), "identB"], [("ps", tb)])
                if b % 8 == 7:
                    cp("act", V(ring, 0, 128, 2048 + (b - 7) * 128, [[128, 8], [1, 128]]), PB(tb, 0, 128, 0, [[128, 8], [1, 128]]), [("ps", tb)], [("ring", 1)])

        def sample_cache_v(l):
            for g in range(2):
                dma("pool", bass.AP(tmpAb.tensor, g * 64, [[TAF, 128], [128, 16], [1, 64]]),
                    cv.ap()[l][:, :, g * 64:(g + 1) * 64].rearrange("b k d -> k b d"), (), VCK)

        def sample_conv_state(l):
            dma("sp", sconvS, sconv.ap()[l], (), ["tmpA"])
            bk = bank()
            for cc in range(4):
                tr(P(bk, 0, 128, cc * 32, [[1, 32]]), tmpA[0:32, cc * 128:(cc + 1) * 128], identF[0:32, 0:32], ["tmpA", "identF"], [("ps", bk)])
            for cc in range(4):
                cp("act", uTs[cc].v(0, 2), P(bk, 0, 128, cc * 32, [[2, 16], [1, 2]]), [("ps", bk)], UTSK)

        def sample_attn(l, rk, r_cur, zi):
            state["l"] = l
            sbn = {}
            for g in range(2):
                bk = bank()
                sbn[g] = bk
                mm(P(bk, 0, 128, 0, [[1, 512]]), kT[rk][g * 64:(g + 1) * 64, :], V(qT, g * 64, 64, 0, [[1, 512]]), True, True,
                   [("kT", rk), "qT"], [("ps", bk)])
            for g in range(2):
                exp_mask(sbn[g], 1, g, V(msnew, 0, 128, 0, [[0, 4], [1, 128]]), "msnew")
            cp("act", V(qn, 0, 128, 0, [[32, 16], [8, 4], [1, 8]]), V(qT, 0, 128, 0, [[8, 16], [128, 4], [1, 8]]), ["qT"], ["qn"])
            sbc = {}
            for g in range(2):
                bk = bank()
                sbc[g] = bk
                for b in range(16):
                    mm(P(bk, 0, 128, b * 32, [[1, 32]]), V(ring, g * 64, 64, 2048 + b * 128, [[1, 128]]),
                       V(qn, g * 64, 64, b * 32, [[1, 32]]), True, True, [("ring", 1), "qn"], [("ps", bk)])
            for g in range(2):
                exp_mask(sbc[g], 0, g, V(mscache, 0, 128, 0, [[0, 64], [1, 8]]), "mscache")

        def sample_attn2(l, rk, r_cur, zi):
            state["l"] = l
            ob = [bank(), bank()]
            for g in range(2):
                for i in range(4):
                    o = P(ob[g], 0, 128, i * 65, [[1, 65]])
                    mm(o, PT[:, 1 * 2 + g, i * 128:(i + 1) * 128], Vaug[r_cur][:, g, 0:65], True, True,
                       [("PT", 1, g), ("Vaug", r_cur)], [("ps", ob[g])])
            oc = [bank(), bank()]
            for g in range(2):
                for b in range(16):
                    mm(P(oc[g], 0, 64, b * 32, [[1, 32]]), Vc_ap(b, g, 64), PT[:, 0 * 2 + g, b * 32:(b + 1) * 32], True, True,
                       [("PT", 0, g)] + VCK, [("ps", oc[g])])
                mm(P(oc[g], 64, 1, 0, [[1, 512]]), Vaug[r_cur][:, 0, 64:65], PT[:, 0 * 2 + g, :], True, True,
                   [("PT", 0, g), ("Vaug", r_cur)], [("ps", oc[g])])
            for g in range(2):
                cp("act", V(tmpB, 0, 65, g * 512, [[128, 4], [8, 16], [1, 8]]), P(oc[g], 0, 65, 0, [[8, 4], [32, 16], [1, 8]]), [("ps", oc[g])], ["tmpB"])
            ot = [bank(), bank()]
            for g in range(2):
                for i in range(4):
                    tr(P(ot[g], 0, 128, i * 65, [[1, 65]]), tmpB[0:65, g * 512 + i * 128:g * 512 + (i + 1) * 128], identF[0:65, 0:65],
                       ["tmpB", "identF"], [("ps", ot[g])])
            for g in range(2):
                cp("act", tmpB[:, g * 260:(g + 1) * 260], P(ot[g], 0, 128, 0, [[1, 260]]), [("ps", ot[g])], ["tmpB"])
                tt("dve", tmpB[:, g * 260:(g + 1) * 260], P(ob[g], 0, 128, 0, [[1, 260]]), tmpB[:, g * 260:(g + 1) * 260], ALU.add,
                   [("ps", ob[g]), "tmpB"], ["tmpB"])
            osrc = [V(tmpB, 0, 128, g * 260, [[65, 4], [1, 64]]) for g in range(2)]
            dsrc = [V(tmpB, 0, 128, g * 260 + 64, [[65, 4]]) for g in range(2)]
            normalize(l, osrc, dsrc, [["tmpB"], ["tmpB"]], zi)

        def out_state_sample(l):
            for b in range(16):
                dma("sp", o_k_s.ap()[l][b, 120:128, :], qk[b * 8:(b + 1) * 8, 512:640], ["qk"], ())
                dma("sp", o_v_s.ap()[l][b, 120:128, :], vout[b * 8:(b + 1) * 8, :], ["vout"], ())

        def out_conv_sample(l):
            for cc in range(4):
                cp("dve", V(tmpB, 0, 128, 896 + cc * 32, [[2, 16], [1, 2]]), uTs[cc].v(8, 2), UTSK, ["tB1"])
            bk = bank()
            for cc in range(4):
                tr(P(bk, 0, 32, cc * 128, [[1, 128]]), tmpB[:, 896 + cc * 32:896 + (cc + 1) * 32], identF[:, :], ["tB1", "identF"], [("ps", bk)])
            cp("act", cvo, P(bk, 0, 32, 0, [[1, 512]]), [("ps", bk)], ["tmpB"])
            dma("sp", o_conv_s.ap()[l], cvo, ["tmpB"], ())

        def out_conv_prompt(l):
            for cc in range(4):
                cp("dve", tmpB[:, 896 + cc * 32:896 + cc * 32 + 2], uT[cc][:, state["ulast"]:state["ulast"] + 2], [("uT", cc)], ["tB1"])
            bk = bank()
            for cc in range(4):
                tr(P(bk, 0, 2, cc * 128, [[1, 128]]), tmpB[:, 896 + cc * 32:896 + cc * 32 + 2], identF[:, :], ["tB1", "identF"], [("ps", bk)])
            cp("act", tmpB[0:2, 512:1024], P(bk, 0, 2, 0, [[1, 512]]), [("ps", bk)], ["tmpB"])
            dma("sp", o_conv_p.ap()[l], tmpB[0:2, 512:1024], ["tmpB"], ())

        for s6 in range(4):
            mod0_dma(s6)
        load_w_piece(0, "kv")
        stats_a(0, 0, 0); stats_a(0, 1, 1)
        for nm in ("q", "za"):
            load_w_piece(0, nm)
        dma("sp", xap(5), xm_ap[3 * 128:4 * 128, :], (), [xkey(5)])
        for s in range(8):
            mod0_mm(s)
        finish_mod(0)

        class Layer:
            pass

        def make_layer(l):
            L = Layer()
            seq = []
            pi = 0 if l == 0 else 1
            if l == 0:
                seq.append(dict(blk=0, kv=True, pi=pi, j=0, pn=2))
                seq.append(dict(blk=1, kv=False, pi=pi, j=1, pn=2))
            else:
                seq.append(dict(blk=1, kv=True, pi=pi, j=0, pn=1))
            pi += 1
            for sbi in range(NBLK // SB):
                for j in range(SB):
                    seq.append(dict(blk=2 + sbi * SB + j, kv=False, pi=pi, j=j, pn=SB))
                pi += 1
            n = len(seq)
            for i, d in enumerate(seq):
                d["i"] = i
            equeue = []
            mod_todo = list(range(12)) if l == 0 else []
            SLOT_S = (seq[n - 1]["pi"] + 1) % 3

            def blk_of(i):
                return 18 if i == n else seq[i]["blk"]

            def do_stats(i, late=False):
                if i <= n:
                    if not late:
                        stats_a(l, blk_of(i), i % 2)
                    stats_b(l, blk_of(i), i % 2)
                    if i < n and l == 0 and seq[i]["blk"] == 0:
                        dma("sp", xap(6), xm_ap[4 * 128:5 * 128, :], (), [xkey(6)])

            def do_atr(i):
                if i < n:
                    a_tr(l, seq[i]["pi"] % 3, seq[i]["j"], False)
                elif i == n:
                    a_tr(l, SLOT_S, 0, True)

            def do_bmm(i):
                if i < n:
                    d = seq[i]
                    b_mm(l, d["blk"], d["pi"] % 3, d["j"], i % 3, i % 2, False, d["kv"], d["blk"] == 17)
                    if d["blk"] == 17:
                        dma("sp", o_k_p.ap()[l], qk[:, 512:640], ["qk"], ())
                        dma("sp", o_v_p.ap()[l], vout[:, :], ["vout"], ())
                elif i == n:
                    b_mm(l, 18, SLOT_S, 0, n % 3, n % 2, True, False, True)
                    out_state_sample(l)
                    if l == 0:
                        for nm in ("kv", "q", "za"):
                            load_w_piece(1, nm)

            def do_qktr(i):
                if i < n:
                    qk_tr(i % 2, seq[i]["kv"])
                elif i == n:
                    qk_tr(n % 2, False)

            def do_sexp(i):
                if i < n and not seq[i]["kv"]:
                    s_exp(l, (i - 1) % 2, i % 2, seq[i]["blk"] == 2)

            def prologue_Ak(k):
                do_stats(k, late=(l == 0 and k < 2))
                if k < n:
                    a_tr(l, seq[k]["pi"] % 3, seq[k]["j"], False, tb=(1 if k == 0 else bank()))

            def prologue_A():
                for k in range(3):
                    prologue_Ak(k)

            def prologue_B():
                do_bmm(0); do_qktr(0); do_bmm(1)

            def prologue_C():
                do_sexp(0); do_stats(3)

            def loop():
                state["ulast"] = 0
                if l == 0:
                    mod_dma(1, 0)
                for i, d in enumerate(seq):
                    if l == 0 and mod_todo and i >= 1 and mod_todo[0] + 1 < 12:
                        mod_dma(1, mod_todo[0] + 1)
                    if not d["kv"]:
                        pv_norm(l, (i - 1) % 3, i % 3, i % 2)
                    do_qktr(i + 1)
                    do_bmm(i + 2)
                    do_atr(i + 3)
                    do_sexp(i + 1)
                    if i == n - 1:
                        sample_attn(l, n % 2, n % 3, n % 2)
                    if i + 4 <= n:
                        stats_a(l, blk_of(i + 4), (i + 4) % 2)
                    while equeue and equeue[0]["i"] < i:
                        eb = equeue.pop(0)
                        e_proj(l, eb["blk"], eb["j"], False)
                    if d["j"] == d["pn"] - 1:
                        first_main_pair = (seq[i]["blk"] == 3)
                        conv(l, d["pi"] % 3, d["pn"], False, u_only=all(seq[k]["kv"] for k in range(i - d["pn"] + 1, i + 1)),
                             halo_to_main=first_main_pair)
                        for k in range(i - d["pn"] + 1, i + 1):
                            if not seq[k]["kv"]:
                                equeue.append(seq[k])
                    if l == 0 and mod_todo and i >= 1:
                        mod_mm(1, mod_todo.pop(0))
                        if not mod_todo:
                            finish_mod(1)
                    if not d["kv"]:
                        y_tr(d["j"])
                    do_stats(i + 4, late=True)
                    if l == 0 and i == 1:
                        state["late_setup"]()
                    if i == n - 5:
                        sample_cache_k_dma(l)
                    if i == n - 3:
                        sample_cache_k_tr(l)
                while equeue:
                    eb = equeue.pop(0)
                    e_proj(l, eb["blk"], eb["j"], False)
                out_conv_prompt(l)

            def tail(nxt):
                sample_cache_v(l)
                if nxt is not None:
                    layer_params(l + 1)
                    build_gate(l + 1, False)
                    nxt.prologue_Ak(0)
                if nxt is not None:
                    nxt.prologue_Ak(1)
                sample_attn2(l, n % 2, n % 3, n % 2)
                sample_conv_state(l)
                if nxt is not None:
                    nxt.prologue_Ak(2)
                y_tr(0)
                if nxt is not None:
                    nxt.prologue_B()
                conv(l, SLOT_S, 1, True)
                out_conv_sample(l)
                if l == 0:
                    for nm in ("cv1", "cv2", "cv3", "cv0"):
                        load_w_piece(1, nm)
                build_gate(l, True)
                e_proj(l, 18, 0, True)
                if l == 0:
                    load_w_piece(1, "out")
                if nxt is not None:
                    nxt.prologue_C()

            L.prologue_A, L.prologue_Ak, L.prologue_B, L.prologue_C, L.loop, L.tail = prologue_A, prologue_Ak, prologue_B, prologue_C, loop, tail
            return L

        LY = [make_layer(0), make_layer(1)]
        LY[0].prologue_A(); LY[0].prologue_B(); LY[0].prologue_C()
        layer_params(0)
        for s6 in range(4, 6):
            mod0_dma(s6)
        for nm in ("cv1", "cv2", "cv3", "cv0", "out"):
            load_w_piece(0, nm)
        def late_setup():
            for s8 in range(8, 12):
                mod0_mm(s8)
            for b in range(7, 18):
                dma("sp", xap(b), xm_ap[(b - 2) * 128:(b - 1) * 128, :], (), [xkey(b)])
            dma("sp", xap(18), xs.ap(), (), [xkey(18)])
            build_gate(0, False)
        state["late_setup"] = late_setup
        LY[0].loop()
        LY[0].tail(LY[1])
        LY[1].loop()
        LY[1].tail(None)

        S.emit()
    return nc


_NC_CACHE = {}


def _tables(half):
    inv = (np.float32(500000.0) ** (-np.arange(0, 16, 2, dtype=np.float32) / np.float32(16))).astype(np.float32)
    p = np.arange(128, dtype=np.float32)
    cosp = np.zeros((128, 18, 8), np.float32); sinp = np.zeros((128, 18, 16), np.float32)
    for blk in range(18):
        pos = np.float32(half * 2048 - 256 + blk * 128) + p
        pos = np.maximum(pos, 0).astype(np.float32)
        ang = (pos[:, None] * inv[None, :]).astype(np.float32)
        cosp[:, blk, :] = np.cos(ang); sinp[:, blk, 0:8] = -np.sin(ang); sinp[:, blk, 8:16] = np.sin(ang)
    t = (np.arange(128) % 8).astype(np.float32)
    pos = (np.float32(8192.0) + t).astype(np.float32)
    ang = (pos[:, None] * inv[None, :]).astype(np.float32)
    sn = np.sin(ang).astype(np.float32)
    return cosp.reshape(128, 144), sinp.reshape(128, 288), np.cos(ang).astype(np.float32), np.concatenate([-sn, sn], axis=1).astype(np.float32)


def kernel(x_prompt, x_sample, c_prompt, c_sample, state_conv, cache_k_win, cache_v_win,
           w_mod, b_mod, norm_g, w_in, conv_w, q_norm_g, k_norm_g, sinks, w_out):
    f = lambda a: np.ascontiguousarray(np.asarray(a, dtype=np.float32))
    x_prompt, x_sample, c_prompt, c_sample = f(x_prompt), f(x_sample), f(c_prompt), f(c_sample)
    state_conv, cache_k_win, cache_v_win = f(state_conv), f(cache_k_win), f(cache_v_win)
    shared = dict(w_mod=f(w_mod), b_mod=f(b_mod), norm_g=f(norm_g), w_in=f(w_in), conv_w=f(conv_w),
                  q_norm_g=f(q_norm_g), k_norm_g=f(k_norm_g), sinks=f(sinks), w_out=f(w_out))
    if "nc" not in _NC_CACHE:
        _NC_CACHE["nc"] = build_nc()
    nc = _NC_CACHE["nc"]
    jj = np.arange(128)[:, None]; ii = np.arange(128)[None, :]
    m_prev = (jj > ii).astype(np.float32)
    m_cur = (jj <= ii).astype(np.float32)
    m_snew = (((jj // 8) == (ii // 8)) & ((jj % 8) <= (ii % 8))).astype(np.float32)
    m_scache = (np.arange(128)[:, None] > np.arange(8)[None, :]).astype(np.float32)
    ident = np.eye(128, dtype=np.float32)
    in_maps = []
    for core in range(NCORES):
        b, half = core // 2, core % 2
        cosp, sinp, coss, sins = _tables(half)
        s0 = half * 2048
        xh = np.zeros((256, D), np.float32) if half == 0 else x_prompt[b, s0 - 256:s0]
        sb0 = core * 16
        m = dict(shared)
        m.update(
            xh=np.ascontiguousarray(xh), xm=np.ascontiguousarray(x_prompt[b, s0:s0 + 2048]),
            xs=np.ascontiguousarray(x_sample[sb0:sb0 + 16].reshape(128, D)),
            cs=np.ascontiguousarray(np.concatenate([c_prompt[b:b + 1], c_sample[sb0:sb0 + 16]], axis=0)),
            sconv=np.ascontiguousarray(state_conv[:, sb0:sb0 + 16].reshape(2, 32, 512)),
            ck=np.ascontiguousarray(cache_k_win[:, sb0:sb0 + 16].reshape(2, 16, 128, 128)),
            cv=np.ascontiguousarray(cache_v_win[:, sb0:sb0 + 16].reshape(2, 16, 128, 128)),
            cosp=cosp, sinp=sinp, coss=coss, sins=sins,
            m_prev=m_prev, m_cur=m_cur, m_first=(m_prev if half == 1 else np.zeros_like(m_prev)),
            m_snew=m_snew, m_scache=m_scache, ident=ident,
            hvalid=np.full((128, 1), float(half), np.float32),
        )
        in_maps.append(m)
    res = run_bass_kernel_spmd(nc, in_maps, core_ids=list(range(NCORES)))
    R = res.results
    y_prompt = np.zeros((4, 4096, D), np.float32)
    y_sample = np.zeros((128, 8, D), np.float32)
    conv_p = np.zeros((2, 4, 2, 512), np.float32)
    k_p = np.zeros((2, 4, 128, 2, 64), np.float32); v_p = np.zeros_like(k_p)
    conv_s = np.zeros((2, 128, 2, 512), np.float32)
    k_s = np.zeros((2, 128, 128, 2, 64), np.float32); v_s = np.zeros_like(k_s)
    for core in range(NCORES):
        b, half = core // 2, core % 2
        r = R[core]
        y_prompt[b, half * 2048:(half + 1) * 2048] = r["y_main"]
        sb0 = core * 16
        y_sample[sb0:sb0 + 16] = r["y_s"].reshape(16, 8, D)
        conv_s[:, sb0:sb0 + 16] = r["o_conv_s"].reshape(2, 16, 2, 512)
        k_s[:, sb0:sb0 + 16] = r["o_k_s"].reshape(2, 16, 128, 2, 64)
        v_s[:, sb0:sb0 + 16] = r["o_v_s"].reshape(2, 16, 128, 2, 64)
        if half == 1:
            conv_p[:, b] = r["o_conv_p"]
            k_p[:, b] = r["o_k_p"].reshape(2, 128, 2, 64)
            v_p[:, b] = r["o_v_p"].reshape(2, 128, 2, 64)
    return (y_prompt, y_sample, conv_p, k_p, v_p, conv_s, k_s, v_s)
```

```python
import contextlib
import numpy as np
import concourse.bass as bass
import concourse.mybir as mybir
from concourse.bass_utils import run_bass_kernel_spmd

F32 = mybir.dt.float32
BF16 = mybir.dt.bfloat16
ALU = mybir.AluOpType
AF = mybir.ActivationFunctionType
AX = mybir.AxisListType

NCORES = 8
D = 1024
NBLK = 16
NB = 19
SB = 2
EPS = 1e-6
PROJ = 3328
ENGS = ("pe", "act", "dve", "pool", "sp")


class Sched:
    def __init__(self, nc, stack, nlanes_sp=12, nlanes_pool=8):
        self.nc = nc
        self.ops = {e: [] for e in ENGS}
        self.lw = {}
        self.rs = {}
        self.prog = {e: stack.enter_context(nc.semaphore("prog_" + e)) for e in ("pe", "act", "dve", "pool")}
        self.lanes = {
            "sp": [dict(sem=stack.enter_context(nc.semaphore("lsp%d" % i)), count=0, last=None) for i in range(nlanes_sp)],
            "pool": [dict(sem=stack.enter_context(nc.semaphore("lpl%d" % i)), count=0, last=None) for i in range(nlanes_pool)],
        }
        self.lane_rr = {"sp": 0, "pool": 0}
        self.ps_touch = {}
        self.all_dma = []

    ALIAS = {"tmpB": (("tq", 0), ("tq", 1), ("tq", 2), ("tq", 3)), "tB0": (("tq", 0), ("tq", 1), ("tq", 2)), "tB1": (("tq", 2), ("tq", 3)), "tmpA": (("tmpA", 0), ("tmpA", 1))}

    def _expand(self, keys):
        out = []
        for k in keys:
            if isinstance(k, str) and k in self.ALIAS:
                out.extend(self.ALIAS[k])
            else:
                out.append(k)
        return out

    def op(self, eng, fn, reads=(), writes=(), dma=False):
        reads = self._expand(reads)
        writes = self._expand(writes)
        self.nops = getattr(self, "nops", 0) + 1
        for k in list(reads) + list(writes):
            if isinstance(k, tuple) and k and k[0] == "ps":
                self.ps_touch[k[1]] = self.nops
        idx = len(self.ops[eng])
        me = (eng, idx)
        deps = {}
        for r in reads:
            w = self.lw.get(r)
            if w is not None and w != me:
                deps[w] = "raw"
            if isinstance(r, tuple) and r and r[0] == "ps":
                for rd in self.rs.get(r, ()):
                    if rd[0] != eng:
                        deps.setdefault(rd, "rar")
        for r in writes:
            w = self.lw.get(r)
            if w is not None and w != me:
                deps.setdefault(w, "waw")
            for rd in self.rs.get(r, ()):
                if rd != me:
                    deps.setdefault(rd, "war")
        rec = dict(eng=eng, fn=fn, deps=deps, dma=dma, idx=idx, marked=False)
        if dma:
            lanes = self.lanes[eng]
            ln = lanes[self.lane_rr[eng] % len(lanes)]
            self.lane_rr[eng] += 1
            if ln["last"] is not None:
                deps.setdefault(ln["last"], "raw")
            ln["count"] += 16
            rec["lane"] = ln
            rec["val"] = ln["count"]
            ln["last"] = me
            self.all_dma.append(me)
        self.ops[eng].append(rec)
        for r in reads:
            self.rs.setdefault(r, []).append(me)
        for r in writes:
            self.lw[r] = me
            self.rs[r] = []
        return me

    def _filter(self, rec):
        out = []
        for (e, j), kind in rec["deps"].items():
            src = self.ops[e][j]
            if e == rec["eng"] and not src["dma"] and not rec["dma"]:
                if e == "pe":
                    continue
                if kind == "war" and e != "pool":
                    continue
            if e == rec["eng"] and not src["dma"] and rec["dma"] and kind == "war":
                pass
            out.append((e, j))
        return out

    def finalize(self):
        for e in ENGS:
            for rec in self.ops[e]:
                fd = self._filter(rec)
                best, keep = {}, []
                for (fe, fj) in fd:
                    if self.ops[fe][fj]["dma"] or fe == "pool":
                        keep.append((fe, fj))
                    else:
                        best[fe] = max(best.get(fe, -1), fj)
                rec["fdeps"] = keep + [(fe, fj) for fe, fj in best.items()]
                for (fe, fj) in rec["fdeps"]:
                    src = self.ops[fe][fj]
                    if not src["dma"]:
                        src["marked"] = True
        for e in ENGS:
            c = 0
            for rec in self.ops[e]:
                if rec["marked"]:
                    c += 1
                    rec["cnt"] = c

    def emit_engine(self, eng, e):
        seen = {}
        for rec in self.ops[eng]:
            need = {}
            for (fe, fj) in rec["fdeps"]:
                src = self.ops[fe][fj]
                if src["dma"]:
                    sem, val = src["lane"]["sem"], src["val"]
                else:
                    sem, val = self.prog[fe], src["cnt"]
                k = id(sem)
                if k not in need or need[k][1] < val:
                    need[k] = (sem, val)
            for k, (sem, val) in need.items():
                if seen.get(k, 0) >= val:
                    continue
                e.wait_ge(sem, val)
                seen[k] = val
            inst = rec["fn"](e)
            if rec["dma"]:
                inst.then_inc(rec["lane"]["sem"], 16)
            elif rec["marked"]:
                inst.then_inc(self.prog[eng], 1)
        if eng == "sp":
            for q in ("sp", "pool"):
                for ln in self.lanes[q]:
                    if ln["count"] > 0:
                        e.wait_ge(ln["sem"], ln["count"])

    def emit(self):
        self.finalize()
        with self.nc.Block() as block:
            @block.tensor
            def _(e):
                self.emit_engine("pe", e)

            @block.scalar
            def _(e):
                self.emit_engine("act", e)

            @block.vector
            def _(e):
                self.emit_engine("dve", e)

            @block.gpsimd
            def _(e):
                self.emit_engine("pool", e)

            @block.sync
            def _(e):
                self.emit_engine("sp", e)


def _fsz(t):
    n = 1
    for s in list(t.shape)[1:]:
        n *= int(s)
    return n


def V(t, p0, np_, off, dims):
    a = t[:]
    F = a.ap[0][0]
    return bass.AP(a.tensor, p0 * F + off, [[F, np_]] + [list(d) for d in dims])


def build_nc():
    nc = bass.Bass("TRN2", target_bir_lowering=False)
    dram = {}

    def din(name, shape):
        dram[name] = nc.dram_tensor(name, list(shape), F32, kind="ExternalInput")
        return dram[name]

    def dout(name, shape):
        dram[name] = nc.dram_tensor(name, list(shape), F32, kind="ExternalOutput")
        return dram[name]

    xh = din("xh", [256, D]); xm = din("xm", [2048, D]); xs = din("xs", [128, D])
    cs = din("cs", [17, D])
    sconv = din("sconv", [2, 32, 512])
    ck = din("ck", [2, 16, 128, 128]); cv = din("cv", [2, 16, 128, 128])
    w_mod = din("w_mod", [2, D, 3 * D]); b_mod = din("b_mod", [2, 3 * D]); norm_g = din("norm_g", [2, D])
    w_in = din("w_in", [2, D, PROJ]); conv_w = din("conv_w", [2, 3, 512])
    qg = din("q_norm_g", [2, 64]); kg = din("k_norm_g", [2, 64]); sinks = din("sinks", [2, 8])
    w_out = din("w_out", [2, D, D])
    cosp = din("cosp", [128, 18 * 8]); sinp = din("sinp", [128, 18 * 16])
    coss = din("coss", [128, 8]); sins = din("sins", [128, 16])
    m_prev = din("m_prev", [128, 128]); m_cur = din("m_cur", [128, 128]); m_first = din("m_first", [128, 128])
    m_snew = din("m_snew", [128, 128]); m_scache = din("m_scache", [128, 8])
    ident_in = din("ident", [128, 128]); hvalid = din("hvalid", [128, 1])

    y_main = dout("y_main", [2048, D]); y_s = dout("y_s", [128, D])
    o_conv_p = dout("o_conv_p", [2, 2, 512]); o_k_p = dout("o_k_p", [2, 128, 128]); o_v_p = dout("o_v_p", [2, 128, 128])
    o_conv_s = dout("o_conv_s", [2, 32, 512]); o_k_s = dout("o_k_s", [2, 16, 128, 128]); o_v_s = dout("o_v_s", [2, 16, 128, 128])

    stack = contextlib.ExitStack()
    with stack:
        stack.enter_context(nc.allow_non_contiguous_dma(reason="small strided param loads"))
        try:
            stack.enter_context(nc.allow_low_precision(reason="bf16 matmul operands, fp32 accumulate"))
        except Exception:
            pass
        S = Sched(nc, stack)

        def sb(name, shape, dt=F32):
            return stack.enter_context(nc.sbuf_tensor(name, list(shape), dt))

        X = sb("X", [128, NB - 1, D])
        Win = sb("Win", [128, 8, PROJ], BF16)
        Wout = sb("Wout", [128, 8, D], BF16)
        gate_p = sb("gate_p", [128, D])
        modT = [sb("modT%d" % l, [128, 24, 17]) for l in range(2)]
        tmpA = sb("tmpA", [128, D]); tmpB = sb("tmpB", [128, D])
        hs = sb("hs", [128, D], BF16)
        hT = sb("hT", [128, 3, 8, 264], BF16)
        yT = sb("yT", [128, 8, SB * 128], BF16)
        qk = sb("qk", [128, 640]); qn = sb("qn", [128, 640], BF16)
        qT = sb("qT", [128, 4, 128], BF16)
        kT = [sb("kT%d" % i, [128, 128], BF16) for i in range(2)]
        Vaug = [sb("Vaug%d" % i, [128, 2, 66], BF16) for i in range(3)]
        vout = sb("vout", [128, 128])
        szb = [sb("sz%d" % i, [128, 512]) for i in range(2)]
        PT = sb("PT", [128, 4, 512], BF16)
        yattn = sb("yattn", [128, 512], BF16)
        uT = [sb("uT%d" % c, [128, 2 + SB * 128]) for c in range(4)]
        ring = sb("ring", [128, 2, 8, 256], BF16)
        cosT = sb("cosT", [128, 18, 8]); sinT = sb("sinT", [128, 18, 16])
        cosS = sb("cosS", [128, 8]); sinS = sb("sinS", [128, 16])
        mprev = sb("mprev", [128, 128], BF16); mcur = sb("mcur", [128, 128], BF16)
        mfirst = sb("mfirst", [128, 128], BF16); msnew = sb("msnew", [128, 128], BF16)
        mscache = sb("mscache", [128, 8], BF16)
        identF = sb("identF", [128, 128]); identB = sb("identB", [128, 128], BF16)
        gqkL = [sb("gqk%d" % l, [128, 128]) for l in range(2)]
        parT = sb("parT", [128, 2, 44])
        sc17T = sb("sc17T", [128, 8, 17], BF16)
        small = sb("small", [128, 64])
        consts = sb("consts", [128, 16])
        sk2 = sb("sk2", [128, 2, 8])
        PTf = PT[:].bitcast(F32)
        PF = PTf.ap[0][0]

        class _UTS:
            def __init__(self, cc):
                self.cc = cc
            def v(self, t0, n):
                return bass.AP(PTf.tensor, self.cc * 160 + t0, [[PF, 128], [10, 16], [1, n]])
        uTs = [_UTS(c) for c in range(4)]
        UTSK = [("PT", kb, g) for kb in range(2) for g in range(2)]
        parS = V(tmpA, 0, 44, 640, [[128, 2], [1, 128]])
        ust = V(tmpB, 0, 128, 896, [[32, 4], [1, 32]])
        sconvS = tmpA[0:32, 0:512]
        cvo = tmpB[0:32, 512:1024]
        ptq_t = sb("ptq_t", [128, 256])
        hval = small[:, 3:4]
        negM = small[:, 4:6]
        pso = [stack.enter_context(nc.psum_tensor("ps%d" % i, [128, 512], F32)) for i in range(8)]

        rot = {"i": 0, "t": 0}

        def bank():
            k = min((0, 2, 3, 4, 5, 6, 7), key=lambda b: S.ps_touch.get(b, 0))
            S.nops = getattr(S, "nops", 0) + 1
            S.ps_touch[k] = S.nops
            return k

        def tbank():
            return 0

        def P(k, p0, np_, off, dims):
            return V(pso[k], p0, np_, off, dims)

        def PB(k, p0, np_, off, dims):
            a = pso[k][:].bitcast(BF16)
            F = a.ap[0][0]
            return bass.AP(a.tensor, p0 * F + off, [[F, np_]] + [list(d) for d in dims])

        def dma(q, out, in_, reads, writes):
            return S.op(q, lambda e, o=out, i=in_: e.dma_start(out=o, in_=i), reads, writes, dma=True)

        def act(out, in_, func, reads, writes, **kw):
            return S.op("act", lambda e: e.activation(out=out, in_=in_, func=func, **kw), reads, writes)

        def tt(eng, out, in0, in1, op, reads, writes):
            return S.op(eng, lambda e: e.tensor_tensor(out=out, in0=in0, in1=in1, op=op), reads, writes)

        def ts(eng, out, in0, s1, s2, op0, op1, reads, writes):
            return S.op(eng, lambda e: e.tensor_scalar(out=out, in0=in0, scalar1=s1, scalar2=s2, op0=op0, op1=op1), reads, writes)

        def stt(eng, out, in0, sc, in1, op0, op1, reads, writes):
            return S.op(eng, lambda e: e.scalar_tensor_tensor(out=out, in0=in0, scalar=sc, in1=in1, op0=op0, op1=op1), reads, writes)

        def cp(eng, out, in_, reads, writes):
            if eng == "act":
                return act(out, in_, AF.Copy, reads, writes)
            return S.op(eng, lambda e: e.tensor_copy(out=out, in_=in_), reads, writes)

        def mm(out, lhsT, rhs, start, stop, reads, writes):
            return S.op("pe", lambda e: e.matmul(out, lhsT, rhs, start=start, stop=stop), reads, writes)

        def tr(out, in_, ident, reads, writes):
            return S.op("pe", lambda e: e.transpose(out, in_, ident), reads, writes)

        def mset(eng, ap, val, writes):
            return S.op(eng, lambda e: e.memset(ap, val), (), writes)

        def xap(blk):
            if blk == 0:
                return X[:, 5, :]
            return X[:, blk - 1, :]

        def xkey(blk):
            return ("X", 6) if blk == 0 else ("X", blk)

        xh_ap = xh.ap(); xm_ap = xm.ap()
        dma("sp", xap(0), xh_ap[0:128, :], (), [xkey(0)])
        dma("sp", xap(1), xh_ap[128:256, :], (), [xkey(1)])
        dma("sp", identF[:, :], ident_in.ap(), (), ["identF"])
        cp("dve", identB[:, :], identF[:, :], ["identF"], ["identB"])
        cs17 = X[0:17, 4, :]
        csb = hs[0:17, :]
        dma("sp", cs17, cs.ap(), (), [("X", 5)])
        for l in range(2):
            dma("sp", V(tmpA, 0, 24, 640 + l * 128, [[1, 128]]), b_mod.ap()[l].rearrange("(j p) -> j p", p=128), (), [("parS", l, 0)])
            dma("sp", V(tmpA, 24, 8, 640 + l * 128, [[1, 128]]), norm_g.ap()[l].rearrange("(c p) -> c p", p=128), (), [("parS", l, 1)])
            dma("sp", V(tmpA, 32, 12, 640 + l * 128, [[1, 128]]), conv_w.ap()[l].rearrange("i (c p) -> (i c) p", p=128), (), [("parS", l, 2)])
            dma("sp", gqkL[l][:, 0:64], bass.AP(qg.ap().tensor, l * 64, [[0, 128], [1, 64]]), (), [("gqk", l, 0)])
            dma("sp", gqkL[l][:, 64:128], bass.AP(kg.ap().tensor, l * 64, [[0, 128], [1, 64]]), (), [("gqk", l, 1)])
            dma("sp", sk2[:, l, :], bass.AP(sinks.ap().tensor, l * 8, [[0, 128], [1, 8]]), (), [("sk2", l)])
        for b in range(2, 5):
            dma("sp", xap(b), xm_ap[(b - 2) * 128:(b - 1) * 128, :], (), [xkey(b)])
        for k, (dst, src, nm) in enumerate(((mprev, m_prev, "mprev"), (mcur, m_cur, "mcur"), (mfirst, m_first, "mfirst"), (msnew, m_snew, "msnew"))):
            dma("sp", tmpA[:, k * 128:(k + 1) * 128], src.ap(), (), [("mk", k)])
            cp("dve", dst[:, :], tmpA[:, k * 128:(k + 1) * 128], [("mk", k), "tmpA"], [nm])
        dma("sp", tmpA[:, 512:520], m_scache.ap(), (), [("mk", 4)])
        cp("dve", mscache[:, :], tmpA[:, 512:520], [("mk", 4), "tmpA"], ["mscache"])
        dma("sp", cosT[:, :, :], cosp.ap().rearrange("p (b j) -> p b j", j=8), (), ["cosT"])
        dma("sp", sinT[:, :, :], sinp.ap().rearrange("p (b j) -> p b j", j=16), (), ["sinT"])
        dma("sp", cosS[:, :], coss.ap(), (), ["cosS"])
        dma("sp", sinS[:, :], sins.ap(), (), ["sinS"])
        dma("sp", small[:, 3:4], hvalid.ap(), (), ["hval"])
        mset("dve", consts[:, 0:10], -0.5, ["consts"])
        mset("dve", consts[:, 10:11], 0.5, ["consts"])
        for i in range(3):
            mset("pool", Vaug[i][:, :, 64:66], 2.0, [("Vaug", i)])
        for c in range(4):
            mset("pool", uT[c][:, :], 0.0, [("uT", c)])
        for l in range(2):
            bk = bank()
            tr(P(bk, 0, 128, 0, [[1, 44]]), V(tmpA, 0, 44, 640 + l * 128, [[1, 128]]), identF[0:44, 0:44], [("parS", l, 0), ("parS", l, 1), ("parS", l, 2), "identF"], [("ps", bk)])
            cp("act", parT[:, l, :], P(bk, 0, 128, 0, [[1, 44]]), [("ps", bk), "tmpA"], [("parT", l)])

        act(tmpB[0:17, :], cs17, AF.Tanh, [("X", 5)], ["tmpB"], scale=0.5)
        stt("dve", cs17, tmpB[0:17, :], 1.0, cs17, ALU.add, ALU.mult, ["tmpB", ("X", 5)], [("X", 5)])
        ts("dve", csb, cs17, 0.5, None, ALU.mult, ALU.bypass, [("X", 5)], ["hs"])
        tb = tbank()
        for c in range(8):
            tr(PB(tb, 0, 128, c * 32, [[1, 17]]), hs[0:17, c * 128:(c + 1) * 128], identB[0:17, 0:17], ["hs", "identB"], [("ps", tb)])
        cp("act", sc17T[:, :, :], PB(tb, 0, 128, 0, [[32, 8], [1, 17]]), [("ps", tb)], ["sc17T"])

        def load_w_piece(l, name):
            wl = w_in.ap()[l]
            if name == "q":
                for i in range(4):
                    for g in range(2):
                        h = g * 4 + i
                        src = wl[:, 2048 + h * 64:2048 + (h + 1) * 64].rearrange("(c p) n -> p c n", p=128)
                        c0 = 2048 + i * 128 + g * 64
                        dma("pool", Win[:, :, c0:c0 + 64], src, (), [("W", "q", h)])
            elif name == "kv":
                src = wl[:, 2560:2816].rearrange("(c p) n -> p c n", p=128)
                dma("pool", Win[:, :, 2560:2816], src, (), [("W", "kv")])
            elif name == "za":
                src = wl[:, 2816:3328].rearrange("(c p) n -> p c n", p=128)
                dma("pool", Win[:, :, 2816:3328], src, (), [("W", "za")])
            elif name.startswith("cv"):
                qi = int(name[2])
                src = wl[:, qi * 512:(qi + 1) * 512].rearrange("(c p) n -> p c n", p=128)
                dma("pool", Win[:, :, qi * 512:(qi + 1) * 512], src, (), [("W", name)])
            elif name == "out":
                for hf in range(2):
                    src = w_out.ap()[l][:, hf * 512:(hf + 1) * 512].rearrange("(c p) n -> p c n", p=128)
                    dma("pool", Wout[:, :, hf * 512:(hf + 1) * 512], src, (), [("W", "out", hf)])

        Xbf = X[:].bitcast(BF16)
        XF = Xbf.ap[0][0]
        WM0 = 6 * 2048
        WMK = [("X", b) for b in range(7, 19)]

        def mod0_dma(s6):
            src = w_mod.ap()[0][:, s6 * 512:(s6 + 1) * 512].rearrange("(c p) n -> p c n", p=128)
            dst = bass.AP(Xbf.tensor, WM0 + s6 * 512, [[XF, 128], [3072, 8], [1, 512]])
            dma("pool", dst, src, (), [("wm0", s6)])

        def mod0_mm(s):
            bk = bank()
            for jj in range(2):
                j = 2 * s + jj
                for c in range(8):
                    lhs = bass.AP(Xbf.tensor, WM0 + c * 3072 + j * 128, [[XF, 128], [1, 128]])
                    mm(P(bk, 0, 128, jj * 17, [[1, 17]]), lhs, sc17T[:, c, :], c == 0, c == 7, WMK + [("wm0", s // 2), "sc17T"], [("ps", bk)])
            j0 = 2 * s
            tt("dve", modT[0][:, j0:j0 + 2, :], P(bk, 0, 128, 0, [[17, 2], [1, 17]]),
               V(parT, 0, 128, j0, [[1, 2], [0, 17]]), ALU.add, [("ps", bk), ("parT", 0)], [("modT", 0, s)])

        def mod_dma(l, s):
            slot = s % 2
            src = w_mod.ap()[l][:, s * 256:(s + 1) * 256].rearrange("(c p) n -> p c n", p=128)
            dma("pool", ring[:, slot, :, :], src, (), [("ring", slot)])

        def mod_mm(l, s):
            slot = s % 2
            bk = bank()
            for jj in range(2):
                for c in range(8):
                    mm(P(bk, 0, 128, jj * 17, [[1, 17]]), ring[:, slot, c, jj * 128:(jj + 1) * 128], sc17T[:, c, :],
                       c == 0, c == 7, [("ring", slot), "sc17T"], [("ps", bk)])
            j0 = 2 * s
            tt("dve", modT[l][:, j0:j0 + 2, :], P(bk, 0, 128, 0, [[17, 2], [1, 17]]),
               V(parT, 0, 128, l * 44 + j0, [[1, 2], [0, 17]]), ALU.add, [("ps", bk), ("parT", l)], [("modT", l, s)])

        def layer_params(l):
            gqk = gqkL[l]
            gk_ = [("gqk", l, 0), ("gqk", l, 1)]
            tt("dve", tmpB[:, 0:64], gqk[:, 0:64], gqk[:, 0:64], ALU.mult, gk_, ["tmpB"])
            S.op("dve", lambda e: e.tensor_reduce(out=small[:, 40:41], in_=tmpB[:, 0:64], axis=AX.X, op=ALU.max), ["tmpB"], ["small_m"])
            tt("dve", tmpB[:, 0:64], gqk[:, 64:128], gqk[:, 64:128], ALU.mult, gk_ + ["small_m"], ["tmpB"])
            S.op("dve", lambda e: e.tensor_reduce(out=small[:, 41:42], in_=tmpB[:, 0:64], axis=AX.X, op=ALU.max), ["tmpB"], ["small_m2"])
            tt("dve", small[:, 42:43], small[:, 40:41], small[:, 41:42], ALU.mult, ["small_m", "small_m2"], ["small_m3"])
            tt("pool", small[:, 43:44], small[:, 42:43], consts[:, 10:11], ALU.pow, ["small_m3", "consts"], ["small_m4"])
            ts("dve", small[:, 4 + l:5 + l], small[:, 43:44], -8.0, None, ALU.mult, ALU.bypass, ["small_m4"], [("negM", l)])
            act(sk2[:, l, :], sk2[:, l, :], AF.Exp, [("sk2", l), ("negM", l)], [("sk2", l)], bias=small[:, 4 + l:5 + l], scale=1.0)
            ts("dve", sk2[:, l, :], sk2[:, l, :], 2.0, None, ALU.mult, ALU.bypass, [("sk2", l)], [("sk2", l)])

        def finish_mod(l):
            mk = [("modT", l, s) for s in range(4, 8)]
            stt("dve", modT[l][:, 8:16, :], modT[l][:, 8:16, :], 1.0, V(parT, 0, 128, l * 44 + 24, [[1, 8], [0, 17]]), ALU.add, ALU.mult,
                mk + [("parT", l)], [("modT", l, "gs")])

        def build_gate(l, sample):
            mk = [("modT", l, s) for s in range(12)]
            if sample:
                cp("dve", V(tmpB, 0, 128, 0, [[128, 8], [8, 16], [1, 8]]), V(modT[l], 0, 128, 16 * 17 + 1, [[17, 8], [1, 16], [0, 8]]), mk, ["tmpB"])
            dst, nm = (tmpA, "tmpA") if sample else (gate_p, "gate_p")
            for hf in range(2):
                bk = bank()
                for c4 in range(4):
                    c = hf * 4 + c4
                    if not sample:
                        lhs = V(modT[l], 0, 128, (16 + c) * 17, [[0, 128]])
                        rk = mk
                    else:
                        lhs = tmpB[:, c * 128:(c + 1) * 128]
                        rk = ["tmpB"]
                    mm(P(bk, 0, 128, c4 * 128, [[1, 128]]), lhs, identF[:, :], True, True, rk + ["identF"], [("ps", bk)])
                cp("act", dst[:, hf * 512:(hf + 1) * 512], P(bk, 0, 128, 0, [[1, 512]]), [("ps", bk)], [nm])

        state = {"l": 0, "ulast": 0}
        ROPE0 = 640

        def HK(pair, j):
            return [("hT", pair, j, c) for c in range(8)]

        def hT_ap(slot, c, j0, n):
            return V(hT, 0, 128, slot * 8 * 264 + c * 264 + j0 * 128, [[1, n]])

        def stats_a(l, blk, par):
            xa = xap(blk)
            c0 = 6 * par
            act(V(small, 0, 128, 44, [[0, D]]), xa, AF.Square, [xkey(blk)], ["junk", ("ss", par)], accum_out=small[:, c0:c0 + 1])
            ts("dve", small[:, c0 + 1:c0 + 2], small[:, c0:c0 + 1], 1.0 / D, EPS, ALU.mult, ALU.add, [("ss", par)], [("ms", par)])
            tt("pool", small[:, c0 + 2:c0 + 3], small[:, c0 + 1:c0 + 2], consts[:, 0:1], ALU.pow, [("ms", par), "consts"], [("rstd", par)])

        def stats_b(l, blk, par):
            c0 = 6 * par
            act(hs[:, :], xap(blk), AF.Identity, [xkey(blk), ("rstd", par)], ["hs"], scale=small[:, c0 + 2:c0 + 3])

        def stats(l, blk):
            stats_a(l, blk, 0)
            stats_b(l, blk, 0)

        def a_tr(l, pair, j, sample, tb=1):
            for c in range(8):
                tr(PB(tb, 0, 128, c * 128, [[1, 128]]), hs[:, c * 128:(c + 1) * 128], identB[:, :], ["hs", "identB"], [("ps", tb)])
            mkeys = [("modT", l, s) for s in range(4)] + [("modT", l, "gs")]
            hk = HK(pair, j)
            if not sample:
                for c in range(8):
                    act(hT_ap(pair, c, j, 128), PB(tb, 0, 128, c * 128, [[1, 128]]), AF.Identity, [("ps", tb)] + mkeys, [hk[c]],
                        scale=modT[l][:, 8 + c, 0:1], bias=modT[l][:, c, 0:1])
            else:
                tt("dve", V(tmpB, 0, 128, 0, [[128, 8], [8, 16], [1, 8]]), PB(tb, 0, 128, 0, [[128, 8], [8, 16], [1, 8]]),
                   V(modT[l], 0, 128, 8 * 17 + 1, [[17, 8], [1, 16], [0, 8]]), ALU.mult, [("ps", tb)] + mkeys, ["tmpB"])
                tt("dve", V(hT, 0, 128, pair * 8 * 264 + j * 128, [[264, 8], [8, 16], [1, 8]]), V(tmpB, 0, 128, 0, [[128, 8], [8, 16], [1, 8]]),
                   V(modT[l], 0, 128, 1, [[17, 8], [1, 16], [0, 8]]), ALU.add, ["tmpB"] + mkeys, hk)

        def b_mm(l, blk, pair, j, r3, zi, sample, kv_only, want_v32):
            hk = HK(pair, j)
            if not kv_only:
                bq = bank()
                for c in range(8):
                    mm(P(bq, 0, 128, 0, [[1, 512]]), hT_ap(pair, c, j, 128), Win[:, c, 2048:2560], c == 0, c == 7, hk + [("W", "q", hh) for hh in range(8)], [("ps", bq)])
            bkv = bank()
            for c in range(8):
                mm(P(bkv, 0, 128, 0, [[1, 256]]), hT_ap(pair, c, j, 128), Win[:, c, 2560:2816], c == 0, c == 7, hk + [("W", "kv")], [("ps", bkv)])
            if not kv_only:
                bza = bank()
                for c in range(8):
                    mm(P(bza, 0, 128, 0, [[1, 512]]), hT_ap(pair, c, j, 128), Win[:, c, 2816:3328], c == 0, c == 7, hk + [("W", "za")], [("ps", bza)])
                cp("act", qk[:, 0:512], P(bq, 0, 128, 0, [[1, 512]]), [("ps", bq)], ["qk"])
            cp("act", qk[:, 512:640], P(bkv, 0, 128, 0, [[1, 128]]), [("ps", bkv)], ["qk"])
            cp("act", Vaug[r3][:, :, 0:64], P(bkv, 0, 128, 128, [[64, 2], [1, 64]]), [("ps", bkv)], [("Vaug", r3)])
            if want_v32:
                cp("act", vout[:, :], P(bkv, 0, 128, 128, [[1, 128]]), [("ps", bkv)], ["vout"])
            h0, nh = (8, 2) if kv_only else (0, 10)
            c0 = h0 * 64
            ncol = nh * 64
            if not kv_only:
                act(szb[zi][:, :], P(bza, 0, 128, 0, [[1, 512]]), AF.Tanh, [("ps", bza)], [("sz", zi)], scale=0.5)
                stt("dve", szb[zi][:, :], szb[zi][:, :], 1.0, P(bza, 0, 128, 0, [[1, 512]]), ALU.add, ALU.mult, [("sz", zi), ("ps", bza)], [("sz", zi)])
            tt("dve", tmpB[:, c0:c0 + ncol], qk[:, c0:c0 + ncol], qk[:, c0:c0 + ncol], ALU.mult, ["qk"], ["tB0"])
            S.op("dve", lambda e: e.tensor_reduce(out=small[:, 10 + h0:10 + h0 + nh], in_=V(tmpB, 0, 128, c0, [[64, nh], [1, 64]]),
                                                  axis=AX.X, op=ALU.add), ["tB0"], ["ssq"])
            ts("dve", small[:, 20 + h0:20 + h0 + nh], small[:, 10 + h0:10 + h0 + nh], 1.0 / 64, EPS, ALU.mult, ALU.add, ["ssq"], ["msq"])
            tt("pool", small[:, 30 + h0:30 + h0 + nh], small[:, 20 + h0:20 + h0 + nh], consts[:, h0:h0 + nh], ALU.pow, ["msq", "consts"], ["rq"])
            if not kv_only:
                tt("dve", V(qk, 0, 128, 0, [[64, 8], [1, 64]]), V(qk, 0, 128, 0, [[64, 8], [1, 64]]), V(gqkL[l], 0, 128, 0, [[0, 8], [1, 64]]),
                   ALU.mult, ["qk", ("gqk", l, 0)], ["qk"])
            tt("dve", V(qk, 0, 128, 512, [[64, 2], [1, 64]]), V(qk, 0, 128, 512, [[64, 2], [1, 64]]), V(gqkL[l], 0, 128, 64, [[0, 2], [1, 64]]),
               ALU.mult, ["qk", ("gqk", l, 1)], ["qk"])
            tt("dve", V(qk, 0, 128, c0, [[64, nh], [1, 64]]), V(qk, 0, 128, c0, [[64, nh], [1, 64]]),
               V(small, 0, 128, 30 + h0, [[1, nh], [0, 64]]), ALU.mult, ["qk", "rq"], ["qk"])
            if sample:
                cosb = V(cosS, 0, 128, 0, [[0, nh], [0, 2], [1, 8]]); sinb = V(sinS, 0, 128, 0, [[0, nh], [1, 16]])
                tk = ["cosS", "sinS"]
            else:
                cosb = V(cosT, 0, 128, blk * 8, [[0, nh], [0, 2], [1, 8]]); sinb = V(sinT, 0, 128, blk * 16, [[0, nh], [1, 16]])
                tk = ["cosT", "sinT"]
            xr = V(qk, 0, 128, c0, [[64, nh], [8, 2], [1, 8]])
            xsw = V(qk, 0, 128, c0 + 8, [[64, nh], [-8, 2], [1, 8]])
            t1 = V(tmpB, 0, 128, ROPE0, [[16, nh], [8, 2], [1, 8]])
            t2 = V(tmpB, 0, 128, ROPE0 + 160, [[16, nh], [1, 16]])
            tt("dve", t1, xr, cosb, ALU.mult, ["qk"] + tk, ["tB1"])
            sin3 = V(sinS if sample else sinT, 0, 128, 0 if sample else blk * 16, [[0, nh], [8, 2], [1, 8]])
            tt("dve", V(tmpB, 0, 128, ROPE0 + 160, [[16, nh], [8, 2], [1, 8]]), xsw, sin3, ALU.mult, ["qk"] + tk, ["tB1"])
            tt("dve", V(qk, 0, 128, c0, [[64, nh], [1, 16]]), V(tmpB, 0, 128, ROPE0, [[16, nh], [1, 16]]), t2, ALU.add, ["tB1"], ["qk"])
            cp("dve", qn[:, c0:c0 + ncol], qk[:, c0:c0 + ncol], ["qk"], ["qn"])

        def qk_tr(r3, kv_only):
            tb = bank()
            if not kv_only:
                for i in range(4):
                    tr(PB(tb, 0, 128, i * 128, [[1, 128]]), qn[:, i * 128:(i + 1) * 128], identB[:, :], ["qn", "identB"], [("ps", tb)])
            tr(PB(tb, 0, 128, 512, [[1, 128]]), qn[:, 512:640], identB[:, :], ["qn", "identB"], [("ps", tb)])
            if not kv_only:
                cp("act", qT[:, :, :], PB(tb, 0, 128, 0, [[128, 4], [1, 128]]), [("ps", tb)], ["qT"])
            cp("act", kT[r3][:, :], PB(tb, 0, 128, 512, [[1, 128]]), [("ps", tb)], [("kT", r3)])

        def exp_mask(bk, kb, g, maskap, mkey, c0=0, n=512):
            l = state["l"]
            act(PT[:, kb * 2 + g, c0:c0 + n], P(bk, 0, 128, c0, [[1, n]]), AF.Exp, [("ps", bk), ("negM", l)], [("PT", kb, g)],
                bias=small[:, 4 + l:5 + l], scale=0.125)
            tt("dve", PT[:, kb * 2 + g, c0:c0 + n], PT[:, kb * 2 + g, c0:c0 + n], maskap, ALU.mult, [("PT", kb, g), mkey], [("PT", kb, g)])

        def s_exp(l, r_prev, r_cur, first_main):
            state["l"] = l
            sbk = {}
            for g in range(2):
                for kb, kr in ((0, r_prev), (1, r_cur)):
                    bk = bank()
                    sbk[(kb, g)] = bk
                    mm(P(bk, 0, 128, 0, [[1, 512]]), kT[kr][g * 64:(g + 1) * 64, :], V(qT, g * 64, 64, 0, [[1, 512]]), True, True,
                       [("kT", kr), "qT"], [("ps", bk)])
            for g in range(2):
                mp = (mfirst, "mfirst") if first_main else (mprev, "mprev")
                exp_mask(sbk[(0, g)], 0, g, V(mp[0], 0, 128, 0, [[0, 4], [1, 128]]), mp[1])
                exp_mask(sbk[(1, g)], 1, g, V(mcur, 0, 128, 0, [[0, 4], [1, 128]]), "mcur")

        def normalize(l, osrc, dsrc, okeys, zi):
            for g in range(2):
                tt("dve", small[:, 48 + g * 4:52 + g * 4], dsrc[g], sk2[:, l, g * 4:(g + 1) * 4], ALU.add, okeys[g] + [("sk2", l)], [("den", g)])
            S.op("dve", lambda e: e.reciprocal(out=small[:, 56:64], in_=small[:, 48:56]), [("den", 0), ("den", 1)], [("rden", 0), ("rden", 1)])
            tt("dve", V(szb[zi], 0, 128, 0, [[64, 8], [1, 64]]), V(szb[zi], 0, 128, 0, [[64, 8], [1, 64]]), V(small, 0, 128, 56, [[1, 8], [0, 64]]),
               ALU.mult, [("sz", zi), ("rden", 0), ("rden", 1)], [("sz", zi)])
            for g in range(2):
                tt("dve", V(yattn, 0, 128, g * 256, [[64, 4], [1, 64]]), osrc[g], V(szb[zi], 0, 128, g * 256, [[64, 4], [1, 64]]), ALU.mult,
                   okeys[g] + [("sz", zi)], ["yattn"])

        def pv_norm(l, r_prev, r_cur, zi):
            ob = [bank(), bank()]
            for g in range(2):
                for i in range(4):
                    o = P(ob[g], 0, 128, i * 65, [[1, 65]])
                    mm(o, PT[:, 0 * 2 + g, i * 128:(i + 1) * 128], Vaug[r_prev][:, g, 0:65], True, False,
                       [("PT", 0, g), ("Vaug", r_prev)], [("ps", ob[g])])
                    mm(o, PT[:, 1 * 2 + g, i * 128:(i + 1) * 128], Vaug[r_cur][:, g, 0:65], False, True,
                       [("PT", 1, g), ("Vaug", r_cur)], [("ps", ob[g])])
            osrc = [P(ob[g], 0, 128, 0, [[65, 4], [1, 64]]) for g in range(2)]
            dsrc = [P(ob[g], 0, 128, 64, [[65, 4]]) for g in range(2)]
            normalize(l, osrc, dsrc, [[("ps", ob[0])], [("ps", ob[1])]], zi)

        def y_tr(j):
            tb = bank()
            for pr in range(4):
                tr(PB(tb, 0, 128, pr * 128, [[1, 128]]), yattn[:, pr * 128:(pr + 1) * 128], identB[:, :], ["yattn", "identB"], [("ps", tb)])
            cp("act", V(yT, 0, 128, 4 * SB * 128 + j * 128, [[SB * 128, 4], [1, 128]]), PB(tb, 0, 128, 0, [[128, 4], [1, 128]]),
               [("ps", tb)], [("yTa", j)])

        def conv(l, pair, nblk, sample, u_only=False, halo_to_main=False):
            N = nblk * 128
            hk = [k for j in range(nblk) for k in HK(pair, j)]
            pend = None
            for cc in range(4):
                p = cc % 2
                HC = 512 + p * 256
                hck = ("tq", 2 + p)
                cak = ("tq", p)
                bkA = bank()
                bkB = None if u_only else bank()
                loc = {1: (bkA, 0), 2: (bkA, 256), 3: (bkB, 0), 0: (bkB, 256)}
                for qi in ((1, 2) if u_only else (1, 2, 3, 0)):
                    bk, o0 = loc[qi]
                    for c in range(8):
                        mm(P(bk, 0, 128, o0, [[1, N]]), Win[:, c, qi * 512 + cc * 128:qi * 512 + (cc + 1) * 128], hT_ap(pair, c, 0, N), c == 0, c == 7,
                           hk + [("W", "cv%d" % qi)], [("ps", bk)])
                cp("act", tmpB[:, HC:HC + N], P(bkA, 0, 128, 256, [[1, N]]), [("ps", bkA)], [hck])
                if not sample:
                    ukey = ("uT", cc)
                    if halo_to_main:
                        ts("dve", uT[cc][:, 0:2], uT[cc][:, state["ulast"]:state["ulast"] + 2], small[:, 3:4], None, ALU.mult, ALU.bypass,
                           [ukey, "hval"], [ukey])
                    else:
                        cp("dve", uT[cc][:, 0:2], uT[cc][:, state["ulast"]:state["ulast"] + 2], [ukey], [ukey])
                    tt("dve", uT[cc][:, 2:2 + N], P(bkA, 0, 128, 0, [[1, N]]), tmpB[:, HC:HC + N], ALU.mult, [("ps", bkA), hck, ukey], [ukey])
                    u0 = uT[cc][:, 0:N]; u1 = uT[cc][:, 1:1 + N]; u2 = uT[cc][:, 2:2 + N]
                    shp = [[1, N]]
                    urk = [ukey]
                else:
                    tt("dve", uTs[cc].v(2, 8), P(bkA, 0, 128, 0, [[8, 16], [1, 8]]), V(tmpB, 0, 128, HC, [[8, 16], [1, 8]]), ALU.mult,
                       [("ps", bkA), hck] + UTSK, UTSK)
                    u0 = uTs[cc].v(0, 8); u1 = uTs[cc].v(1, 8); u2 = uTs[cc].v(2, 8)
                    shp = [[8, 16], [1, 8]]
                    urk = UTSK
                if u_only:
                    continue
                ca = V(tmpB, 0, 128, p * 256, shp)
                tq = V(ptq_t, 0, 128, 0, shp)
                w0 = parT[:, l, 32 + cc:32 + cc + 1]; w1 = parT[:, l, 36 + cc:36 + cc + 1]; w2 = parT[:, l, 40 + cc:40 + cc + 1]
                ts("pool", ca, u0, w0, 0.0, ALU.mult, ALU.add, urk + [("parT", l)], [cak])
                ts("pool", tq, u1, w1, 0.0, ALU.mult, ALU.add, urk + [("parT", l)], ["ptq"])
                tt("pool", ca, ca, tq, ALU.add, [cak, "ptq"], [cak])
                ts("pool", tq, u2, w2, 0.0, ALU.mult, ALU.add, urk + [("parT", l)], ["ptq"])
                tt("pool", ca, ca, tq, ALU.add, [cak, "ptq"], [cak])
                act(tmpB[:, HC:HC + N], P(bkB, 0, 128, 0, [[1, N]]), AF.Tanh, [("ps", bkB)], [hck], scale=0.5)
                stt("dve", tmpB[:, HC:HC + N], tmpB[:, HC:HC + N], 1.0, P(bkB, 0, 128, 0, [[1, N]]), ALU.add, ALU.mult, [hck, ("ps", bkB)], [hck])
                tt("dve", tmpB[:, HC:HC + N], tmpB[:, HC:HC + N], P(bkB, 0, 128, 256, [[1, N]]), ALU.mult, [hck, ("ps", bkB)], [hck])
                if pend is not None:
                    pend()
                def _fin(cc=cc, p=p, HC=HC, cak=cak, hck=hck, ca=ca, u2=u2, w2=w2, urk=urk):
                    stt("dve", yT[:, cc, 0:N], V(tmpB, 0, 128, p * 256, [[1, N]]), 0.5, tmpB[:, HC:HC + N], ALU.mult, ALU.mult, [cak, hck], [("yTc", cc)])
                pend = _fin
            if pend is not None:
                pend()
            if not sample:
                state["ulast"] = N

        def e_proj(l, blk, j, sample):
            yk = [("yTc", cc) for cc in range(4)] + [("yTa", j)]
            for hf in range(2):
                bk = bank()
                for kc in range(8):
                    mm(P(bk, 0, 128, 0, [[1, 512]]), yT[:, kc, j * 128:(j + 1) * 128], Wout[:, kc, hf * 512:(hf + 1) * 512], kc == 0, kc == 7,
                       yk + [("W", "out", hf)], [("ps", bk)])
                xv = X[:, blk - 1, hf * 512:(hf + 1) * 512]
                if not sample:
                    tt("dve", tmpA[:, hf * 512:(hf + 1) * 512], P(bk, 0, 128, 0, [[1, 512]]), gate_p[:, hf * 512:(hf + 1) * 512], ALU.mult,
                       [("ps", bk), "gate_p"], [("tmpA", hf)])
                    tt("pool", xv, tmpA[:, hf * 512:(hf + 1) * 512], xv, ALU.add, [("tmpA", hf), xkey(blk)], [xkey(blk)])
                else:
                    tt("dve", P(bk, 0, 128, 0, [[1, 512]]), P(bk, 0, 128, 0, [[1, 512]]), tmpA[:, hf * 512:(hf + 1) * 512], ALU.mult,
                       [("ps", bk), "tmpA"], [("ps", bk)])
                    tt("dve", xv, P(bk, 0, 128, 0, [[1, 512]]), xv, ALU.add, [("ps", bk), xkey(blk)], [xkey(blk)])
            if l == 1:
                if sample:
                    dma("sp", y_s.ap(), X[:, blk - 1, :], [xkey(blk)], ())
                elif blk >= 2:
                    dma("sp", y_main.ap()[(blk - 2) * 128:(blk - 1) * 128, :], X[:, blk - 1, :], [xkey(blk)], ())

        tmpAb = tmpA[:].bitcast(BF16)
        TAF = tmpAb.ap[0][0]
        VCK = ["tmpA"]

        def Vc_ap(b, g, n):
            return bass.AP(tmpAb.tensor, (b * 2 + g) * 64, [[TAF, 128], [1, n]])

        def sample_cache_k_dma(l):
            dma("pool", V(ring, 0, 128, 0, [[128, 16], [1, 128]]), ck.ap()[l].rearrange("b k d -> k b d"), (), [("ring", 0)])
            dma("sp", o_k_s.ap()[l][:, 0:120, :], ck.ap()[l][:, 8:128, :], (), ())
            dma("sp", o_v_s.ap()[l][:, 0:120, :], cv.ap()[l][:, 8:128, :], (), ())

        def sample_cache_k_tr(l):
            for b in range(16):
                if b % 8 == 0:
                    tb = bank()
                tr(PB(tb, 0, 128, (b % 8) * 128, [[1, 128]]), V(ring, 0, 128, b * 128, [[1, 128]]), identB[:, :], [("ring", 0), "identB"], [("ps", tb)])
                if b % 8 == 7:
                    cp("act", V(ring, 0, 128, 2048 + (b - 7) * 128, [[128, 8], [1, 128]]), PB(tb, 0, 128, 0, [[128, 8], [1, 128]]), [("ps", tb)], [("ring", 1)])

        def sample_cache_v(l):
            for g in range(2):
                dma("pool", bass.AP(tmpAb.tensor, g * 64, [[TAF, 128], [128, 16], [1, 64]]),
                    cv.ap()[l][:, :, g * 64:(g + 1) * 64].rearrange("b k d -> k b d"), (), VCK)

        def sample_conv_state(l):
            dma("sp", sconvS, sconv.ap()[l], (), ["tmpA"])
            bk = bank()
            for cc in range(4):
                tr(P(bk, 0, 128, cc * 32, [[1, 32]]), tmpA[0:32, cc * 128:(cc + 1) * 128], identF[0:32, 0:32], ["tmpA", "identF"], [("ps", bk)])
            for cc in range(4):
                cp("act", uTs[cc].v(0, 2), P(bk, 0, 128, cc * 32, [[2, 16], [1, 2]]), [("ps", bk)], UTSK)

        def sample_attn(l, rk, r_cur, zi):
            state["l"] = l
            sbn = {}
            for g in range(2):
                bk = bank()
                sbn[g] = bk
                mm(P(bk, 0, 128, 0, [[1, 512]]), kT[rk][g * 64:(g + 1) * 64, :], V(qT, g * 64, 64, 0, [[1, 512]]), True, True,
                   [("kT", rk), "qT"], [("ps", bk)])
            for g in range(2):
                exp_mask(sbn[g], 1, g, V(msnew, 0, 128, 0, [[0, 4], [1, 128]]), "msnew")
            cp("act", V(qn, 0, 128, 0, [[32, 16], [8, 4], [1, 8]]), V(qT, 0, 128, 0, [[8, 16], [128, 4], [1, 8]]), ["qT"], ["qn"])
            sbc = {}
            for g in range(2):
                bk = bank()
                sbc[g] = bk
                for b in range(16):
                    mm(P(bk, 0, 128, b * 32, [[1, 32]]), V(ring, g * 64, 64, 2048 + b * 128, [[1, 128]]),
                       V(qn, g * 64, 64, b * 32, [[1, 32]]), True, True, [("ring", 1), "qn"], [("ps", bk)])
            for g in range(2):
                exp_mask(sbc[g], 0, g, V(mscache, 0, 128, 0, [[0, 64], [1, 8]]), "mscache")

        def sample_attn2(l, rk, r_cur, zi):
            state["l"] = l
            ob = [bank(), bank()]
            for g in range(2):
                for i in range(4):
                    o = P(ob[g], 0, 128, i * 65, [[1, 65]])
                    mm(o, PT[:, 1 * 2 + g, i * 128:(i + 1) * 128], Vaug[r_cur][:, g, 0:65], True, True,
                       [("PT", 1, g), ("Vaug", r_cur)], [("ps", ob[g])])
            oc = [bank(), bank()]
            for g in range(2):
                for b in range(16):
                    mm(P(oc[g], 0, 64, b * 32, [[1, 32]]), Vc_ap(b, g, 64), PT[:, 0 * 2 + g, b * 32:(b + 1) * 32], True, True,
                       [("PT", 0, g)] + VCK, [("ps", oc[g])])
                mm(P(oc[g], 64, 1, 0, [[1, 512]]), Vaug[r_cur][:, 0, 64:65], PT[:, 0 * 2 + g, :], True, True,
                   [("PT", 0, g), ("Vaug", r_cur)], [("ps", oc[g])])
            for g in range(2):
                cp("act", V(tmpB, 0, 65, g * 512, [[128, 4], [8, 16], [1, 8]]), P(oc[g], 0, 65, 0, [[8, 4], [32, 16], [1, 8]]), [("ps", oc[g])], ["tmpB"])
            ot = [bank(), bank()]
            for g in range(2):
                for i in range(4):
                    tr(P(ot[g], 0, 128, i * 65, [[1, 65]]), tmpB[0:65, g * 512 + i * 128:g * 512 + (i + 1) * 128], identF[0:65, 0:65],
                       ["tmpB", "identF"], [("ps", ot[g])])
            for g in range(2):
                cp("act", tmpB[:, g * 260:(g + 1) * 260], P(ot[g], 0, 128, 0, [[1, 260]]), [("ps", ot[g])], ["tmpB"])
                tt("dve", tmpB[:, g * 260:(g + 1) * 260], P(ob[g], 0, 128, 0, [[1, 260]]), tmpB[:, g * 260:(g + 1) * 260], ALU.add,
                   [("ps", ob[g]), "tmpB"], ["tmpB"])
            osrc = [V(tmpB, 0, 128, g * 260, [[65, 4], [1, 64]]) for g in range(2)]
            dsrc = [V(tmpB, 0, 128, g * 260 + 64, [[65, 4]]) for g in range(2)]
            normalize(l, osrc, dsrc, [["tmpB"], ["tmpB"]], zi)

        def out_state_sample(l):
            for b in range(16):
                dma("sp", o_k_s.ap()[l][b, 120:128, :], qk[b * 8:(b + 1) * 8, 512:640], ["qk"], ())
                dma("sp", o_v_s.ap()[l][b, 120:128, :], vout[b * 8:(b + 1) * 8, :], ["vout"], ())

        def out_conv_sample(l):
            for cc in range(4):
                cp("dve", V(tmpB, 0, 128, 896 + cc * 32, [[2, 16], [1, 2]]), uTs[cc].v(8, 2), UTSK, ["tB1"])
            bk = bank()
            for cc in range(4):
                tr(P(bk, 0, 32, cc * 128, [[1, 128]]), tmpB[:, 896 + cc * 32:896 + (cc + 1) * 32], identF[:, :], ["tB1", "identF"], [("ps", bk)])
            cp("act", cvo, P(bk, 0, 32, 0, [[1, 512]]), [("ps", bk)], ["tmpB"])
            dma("sp", o_conv_s.ap()[l], cvo, ["tmpB"], ())

        def out_conv_prompt(l):
            for cc in range(4):
                cp("dve", tmpB[:, 896 + cc * 32:896 + cc * 32 + 2], uT[cc][:, state["ulast"]:state["ulast"] + 2], [("uT", cc)], ["tB1"])
            bk = bank()
            for cc in range(4):
                tr(P(bk, 0, 2, cc * 128, [[1, 128]]), tmpB[:, 896 + cc * 32:896 + cc * 32 + 2], identF[:, :], ["tB1", "identF"], [("ps", bk)])
            cp("act", tmpB[0:2, 512:1024], P(bk, 0, 2, 0, [[1, 512]]), [("ps", bk)], ["tmpB"])
            dma("sp", o_conv_p.ap()[l], tmpB[0:2, 512:1024], ["tmpB"], ())

        for s6 in range(4):
            mod0_dma(s6)
        load_w_piece(0, "kv")
        stats_a(0, 0, 0); stats_a(0, 1, 1)
        for nm in ("q", "za"):
            load_w_piece(0, nm)
        dma("sp", xap(5), xm_ap[3 * 128:4 * 128, :], (), [xkey(5)])
        for s in range(8):
            mod0_mm(s)
        finish_mod(0)

        class Layer:
            pass

        def make_layer(l):
            L = Layer()
            seq = []
            pi = 0 if l == 0 else 1
            if l == 0:
                seq.append(dict(blk=0, kv=True, pi=pi, j=0, pn=2))
                seq.append(dict(blk=1, kv=False, pi=pi, j=1, pn=2))
            else:
                seq.append(dict(blk=1, kv=True, pi=pi, j=0, pn=1))
            pi += 1
            for sbi in range(NBLK // SB):
                for j in range(SB):
                    seq.append(dict(blk=2 + sbi * SB + j, kv=False, pi=pi, j=j, pn=SB))
                pi += 1
            n = len(seq)
            for i, d in enumerate(seq):
                d["i"] = i
            equeue = []
            mod_todo = list(range(12)) if l == 0 else []
            SLOT_S = (seq[n - 1]["pi"] + 1) % 3

            def blk_of(i):
                return 18 if i == n else seq[i]["blk"]

            def do_stats(i, late=False):
                if i <= n:
                    if not late:
                        stats_a(l, blk_of(i), i % 2)
                    stats_b(l, blk_of(i), i % 2)
                    if i < n and l == 0 and seq[i]["blk"] == 0:
                        dma("sp", xap(6), xm_ap[4 * 128:5 * 128, :], (), [xkey(6)])

            def do_atr(i):
                if i < n:
                    a_tr(l, seq[i]["pi"] % 3, seq[i]["j"], False)
                elif i == n:
                    a_tr(l, SLOT_S, 0, True)

            def do_bmm(i):
                if i < n:
                    d = seq[i]
                    b_mm(l, d["blk"], d["pi"] % 3, d["j"], i % 3, i % 2, False, d["kv"], d["blk"] == 17)
                    if d["blk"] == 17:
                        dma("sp", o_k_p.ap()[l], qk[:, 512:640], ["qk"], ())
                        dma("sp", o_v_p.ap()[l], vout[:, :], ["vout"], ())
                elif i == n:
                    b_mm(l, 18, SLOT_S, 0, n % 3, n % 2, True, False, True)
                    out_state_sample(l)
                    if l == 0:
                        for nm in ("kv", "q", "za"):
                            load_w_piece(1, nm)

            def do_qktr(i):
                if i < n:
                    qk_tr(i % 2, seq[i]["kv"])
                elif i == n:
                    qk_tr(n % 2, False)

            def do_sexp(i):
                if i < n and not seq[i]["kv"]:
                    s_exp(l, (i - 1) % 2, i % 2, seq[i]["blk"] == 2)

            def prologue_Ak(k):
                do_stats(k, late=(l == 0 and k < 2))
                if k < n:
                    a_tr(l, seq[k]["pi"] % 3, seq[k]["j"], False, tb=(1 if k == 0 else bank()))

            def prologue_A():
                for k in range(3):
                    prologue_Ak(k)

            def prologue_B():
                do_bmm(0); do_qktr(0); do_bmm(1)

            def prologue_C():
                do_sexp(0); do_stats(3)

            def loop():
                state["ulast"] = 0
                if l == 0:
                    mod_dma(1, 0)
                for i, d in enumerate(seq):
                    if l == 0 and mod_todo and i >= 1 and mod_todo[0] + 1 < 12:
                        mod_dma(1, mod_todo[0] + 1)
                    if not d["kv"]:
                        pv_norm(l, (i - 1) % 3, i % 3, i % 2)
                    do_qktr(i + 1)
                    do_bmm(i + 2)
                    do_atr(i + 3)
                    do_sexp(i + 1)
                    if i == n - 1:
                        sample_attn(l, n % 2, n % 3, n % 2)
                    if i + 4 <= n:
                        stats_a(l, blk_of(i + 4), (i + 4) % 2)
                    while equeue and equeue[0]["i"] < i:
                        eb = equeue.pop(0)
                        e_proj(l, eb["blk"], eb["j"], False)
                    if d["j"] == d["pn"] - 1:
                        first_main_pair = (seq[i]["blk"] == 3)
                        conv(l, d["pi"] % 3, d["pn"], False, u_only=all(seq[k]["kv"] for k in range(i - d["pn"] + 1, i + 1)),
                             halo_to_main=first_main_pair)
                        for k in range(i - d["pn"] + 1, i + 1):
                            if not seq[k]["kv"]:
                                equeue.append(seq[k])
                    if l == 0 and mod_todo and i >= 1:
                        mod_mm(1, mod_todo.pop(0))
                        if not mod_todo:
                            finish_mod(1)
                    if not d["kv"]:
                        y_tr(d["j"])
                    do_stats(i + 4, late=True)
                    if l == 0 and i == 1:
                        state["late_setup"]()
                    if i == n - 5:
                        sample_cache_k_dma(l)
                    if i == n - 3:
                        sample_cache_k_tr(l)
                while equeue:
                    eb = equeue.pop(0)
                    e_proj(l, eb["blk"], eb["j"], False)
                out_conv_prompt(l)

            def tail(nxt):
                sample_cache_v(l)
                if nxt is not None:
                    build_gate(l + 1, False)
                    nxt.prologue_Ak(0)
                if nxt is not None:
                    nxt.prologue_Ak(1)
                sample_attn2(l, n % 2, n % 3, n % 2)
                sample_conv_state(l)
                if nxt is not None:
                    nxt.prologue_Ak(2)
                y_tr(0)
                if nxt is not None:
                    nxt.prologue_B()
                conv(l, SLOT_S, 1, True)
                out_conv_sample(l)
                if l == 0:
                    for nm in ("cv1", "cv2", "cv3", "cv0"):
                        load_w_piece(1, nm)
                build_gate(l, True)
                e_proj(l, 18, 0, True)
                if l == 0:
                    load_w_piece(1, "out")
                if nxt is not None:
                    nxt.prologue_C()

            L.prologue_A, L.prologue_Ak, L.prologue_B, L.prologue_C, L.loop, L.tail = prologue_A, prologue_Ak, prologue_B, prologue_C, loop, tail
            return L

        LY = [make_layer(0), make_layer(1)]
        LY[0].prologue_A(); LY[0].prologue_B(); LY[0].prologue_C()
        for l in range(2):
            layer_params(l)
        for s6 in range(4, 6):
            mod0_dma(s6)
        for nm in ("cv1", "cv2", "cv3", "cv0", "out"):
            load_w_piece(0, nm)
        def late_setup():
            for s8 in range(8, 12):
                mod0_mm(s8)
            for b in range(7, 18):
                dma("sp", xap(b), xm_ap[(b - 2) * 128:(b - 1) * 128, :], (), [xkey(b)])
            dma("sp", xap(18), xs.ap(), (), [xkey(18)])
            build_gate(0, False)
        state["late_setup"] = late_setup
        LY[0].loop()
        LY[0].tail(LY[1])
        LY[1].loop()
        LY[1].tail(None)

        S.emit()
    return nc


_NC_CACHE = {}


def _tables(half):
    inv = (np.float32(500000.0) ** (-np.arange(0, 16, 2, dtype=np.float32) / np.float32(16))).astype(np.float32)
    p = np.arange(128, dtype=np.float32)
    cosp = np.zeros((128, 18, 8), np.float32); sinp = np.zeros((128, 18, 16), np.float32)
    for blk in range(18):
        pos = np.float32(half * 2048 - 256 + blk * 128) + p
        pos = np.maximum(pos, 0).astype(np.float32)
        ang = (pos[:, None] * inv[None, :]).astype(np.float32)
        cosp[:, blk, :] = np.cos(ang); sinp[:, blk, 0:8] = -np.sin(ang); sinp[:, blk, 8:16] = np.sin(ang)
    t = (np.arange(128) % 8).astype(np.float32)
    pos = (np.float32(8192.0) + t).astype(np.float32)
    ang = (pos[:, None] * inv[None, :]).astype(np.float32)
    sn = np.sin(ang).astype(np.float32)
    return cosp.reshape(128, 144), sinp.reshape(128, 288), np.cos(ang).astype(np.float32), np.concatenate([-sn, sn], axis=1).astype(np.float32)


def kernel(x_prompt, x_sample, c_prompt, c_sample, state_conv, cache_k_win, cache_v_win,
           w_mod, b_mod, norm_g, w_in, conv_w, q_norm_g, k_norm_g, sinks, w_out):
    f = lambda a: np.ascontiguousarray(np.asarray(a, dtype=np.float32))
    x_prompt, x_sample, c_prompt, c_sample = f(x_prompt), f(x_sample), f(c_prompt), f(c_sample)
    state_conv, cache_k_win, cache_v_win = f(state_conv), f(cache_k_win), f(cache_v_win)
    shared = dict(w_mod=f(w_mod), b_mod=f(b_mod), norm_g=f(norm_g), w_in=f(w_in), conv_w=f(conv_w),
                  q_norm_g=f(q_norm_g), k_norm_g=f(k_norm_g), sinks=f(sinks), w_out=f(w_out))
    if "nc" not in _NC_CACHE:
        _NC_CACHE["nc"] = build_nc()
    nc = _NC_CACHE["nc"]
    jj = np.arange(128)[:, None]; ii = np.arange(128)[None, :]
    m_prev = (jj > ii).astype(np.float32)
    m_cur = (jj <= ii).astype(np.float32)
    m_snew = (((jj // 8) == (ii // 8)) & ((jj % 8) <= (ii % 8))).astype(np.float32)
    m_scache = (np.arange(128)[:, None] > np.arange(8)[None, :]).astype(np.float32)
    ident = np.eye(128, dtype=np.float32)
    in_maps = []
    for core in range(NCORES):
        b, half = core // 2, core % 2
        cosp, sinp, coss, sins = _tables(half)
        s0 = half * 2048
        xh = np.zeros((256, D), np.float32) if half == 0 else x_prompt[b, s0 - 256:s0]
        sb0 = core * 16
        m = dict(shared)
        m.update(
            xh=np.ascontiguousarray(xh), xm=np.ascontiguousarray(x_prompt[b, s0:s0 + 2048]),
            xs=np.ascontiguousarray(x_sample[sb0:sb0 + 16].reshape(128, D)),
            cs=np.ascontiguousarray(np.concatenate([c_prompt[b:b + 1], c_sample[sb0:sb0 + 16]], axis=0)),
            sconv=np.ascontiguousarray(state_conv[:, sb0:sb0 + 16].reshape(2, 32, 512)),
            ck=np.ascontiguousarray(cache_k_win[:, sb0:sb0 + 16].reshape(2, 16, 128, 128)),
            cv=np.ascontiguousarray(cache_v_win[:, sb0:sb0 + 16].reshape(2, 16, 128, 128)),
            cosp=cosp, sinp=sinp, coss=coss, sins=sins,
            m_prev=m_prev, m_cur=m_cur, m_first=(m_prev if half == 1 else np.zeros_like(m_prev)),
            m_snew=m_snew, m_scache=m_scache, ident=ident,
            hvalid=np.full((128, 1), float(half), np.float32),
        )
        in_maps.append(m)
    res = run_bass_kernel_spmd(nc, in_maps, core_ids=list(range(NCORES)))
    R = res.results
    y_prompt = np.zeros((4, 4096, D), np.float32)
    y_sample = np.zeros((128, 8, D), np.float32)
    conv_p = np.zeros((2, 4, 2, 512), np.float32)
    k_p = np.zeros((2, 4, 128, 2, 64), np.float32); v_p = np.zeros_like(k_p)
    conv_s = np.zeros((2, 128, 2, 512), np.float32)
    k_s = np.zeros((2, 128, 128, 2, 64), np.float32); v_s = np.zeros_like(k_s)
    for core in range(NCORES):
        b, half = core // 2, core % 2
        r = R[core]
        y_prompt[b, half * 2048:(half + 1) * 2048] = r["y_main"]
        sb0 = core * 16
        y_sample[sb0:sb0 + 16] = r["y_s"].reshape(16, 8, D)
        conv_s[:, sb0:sb0 + 16] = r["o_conv_s"].reshape(2, 16, 2, 512)
        k_s[:, sb0:sb0 + 16] = r["o_k_s"].reshape(2, 16, 128, 2, 64)
        v_s[:, sb0:sb0 + 16] = r["o_v_s"].reshape(2, 16, 128, 2, 64)
        if half == 1:
            conv_p[:, b] = r["o_conv_p"]
            k_p[:, b] = r["o_k_p"].reshape(2, 128, 2, 64)
            v_p[:, b] = r["o_v_p"].reshape(2, 128, 2, 64)
    return (y_prompt, y_sample, conv_p, k_p, v_p, conv_s, k_s, v_s)
```
